# Optimizing a Trainium2 kernel written in Bass

```python
import math
import jax, jax.numpy as jnp
from jax import lax
import numpy as np

D_MODEL = 2048
BATCH = 8
SEQ = 4096
DEPTH = 2

D_FF = 5632
EPS = 1e-6
ATTN_HEAD_DIM = 64
ATTN_HEADS = D_MODEL // (2 * ATTN_HEAD_DIM)
ATTN_KV_HEADS = ATTN_HEADS // 4
ATTN_GROUP = ATTN_HEADS // ATTN_KV_HEADS
WINDOW = 128
ATTN_BLOCK = 128
REL_BUCKETS = 32
REL_MAX_DIST = 128
CONV_CH = D_MODEL // 2
CONV_WIDTH = 31
GLA_HEADS = 4
GLA_DV_HEAD = D_MODEL // 2 // GLA_HEADS
GLA_DK_HEAD = GLA_DV_HEAD // 2
GLA_CHUNK = 64
GLA_GATE_RANK = 16
GLA_GATE_NORM = 16.0
N_BRANCH = 3
ATTN_Q = ATTN_HEADS * ATTN_HEAD_DIM
ATTN_KV = ATTN_KV_HEADS * ATTN_HEAD_DIM
GLA_K = GLA_HEADS * GLA_DK_HEAD
GLA_V = GLA_HEADS * GLA_DV_HEAD
IN_SIZES = (ATTN_Q, ATTN_KV, ATTN_KV, 2 * CONV_CH, GLA_K, GLA_K, GLA_V, GLA_V, GLA_GATE_RANK, N_BRANCH * D_MODEL)
D_IN = sum(IN_SIZES)
SPLIT_IDX = tuple(int(v) for v in np.cumsum(IN_SIZES)[:-1])

kernel_name = "hybrid_gated_swa_conv_gla_macaron"


def rms_norm(x, g):
    xf = x.astype(jnp.float32)
    y = xf * lax.rsqrt(jnp.mean(xf * xf, axis=-1, keepdims=True) + EPS)
    return (y * g.astype(jnp.float32)).astype(x.dtype)


def layer_norm(x, g, b):
    xf = x.astype(jnp.float32)
    mu = jnp.mean(xf, axis=-1, keepdims=True)
    xc = xf - mu
    y = xc * lax.rsqrt(jnp.mean(xc * xc, axis=-1, keepdims=True) + EPS)
    return (y * g.astype(jnp.float32) + b.astype(jnp.float32)).astype(x.dtype)


def swiglu_ffn(h, w_gate, w_up, w_down):
    a = jnp.einsum('bsd,df->bsf', h, w_gate)
    u = jnp.einsum('bsd,df->bsf', h, w_up)
    return jnp.einsum('bsf,fd->bsd', jax.nn.silu(a) * u, w_down)


def t5_band_bias(rel_bias):
    qi = jnp.arange(ATTN_BLOCK)[:, None]
    kj = jnp.arange(2 * ATTN_BLOCK)[None, :]
    n = jnp.maximum(ATTN_BLOCK + qi - kj, 0)
    max_exact = REL_BUCKETS // 2
    nf = jnp.maximum(n, 1).astype(jnp.float32)
    large = max_exact + (jnp.log(nf / max_exact) / math.log(REL_MAX_DIST / max_exact)
                         * (REL_BUCKETS - max_exact)).astype(jnp.int32)
    large = jnp.minimum(large, REL_BUCKETS - 1)
    bucket = jnp.where(n < max_exact, n, large)
    return jnp.transpose(rel_bias[bucket], (2, 0, 1))


def sliding_window_attention(q, k, v, sink, rel_bias):
    B, S = q.shape[0], q.shape[1]
    nb = S // ATTN_BLOCK
    qb = q.reshape(B, nb, ATTN_BLOCK, ATTN_KV_HEADS, ATTN_GROUP, ATTN_HEAD_DIM)

    def band(t):
        tb = t.reshape(B, nb, ATTN_BLOCK, ATTN_KV_HEADS, ATTN_HEAD_DIM)
        prev = jnp.concatenate([jnp.zeros_like(tb[:, :1]), tb[:, :-1]], axis=1)
        return jnp.concatenate([prev, tb], axis=2)

    kband, vband = band(k), band(v)
    logits = jnp.einsum('bnqhgd,bnkhd->bnhgqk', qb, kband).astype(jnp.float32) * (ATTN_HEAD_DIM ** -0.5)
    bias = t5_band_bias(rel_bias).astype(jnp.float32).reshape(ATTN_KV_HEADS, ATTN_GROUP, ATTN_BLOCK, 2 * ATTN_BLOCK)
    logits = logits + bias[None, None]
    qi = jnp.arange(ATTN_BLOCK)[:, None]
    kj = jnp.arange(2 * ATTN_BLOCK)[None, :]
    dist = ATTN_BLOCK + qi - kj
    in_window = (dist >= 0) & (dist < WINDOW)
    has_prev = (jnp.arange(nb) > 0)[:, None, None] | (kj >= ATTN_BLOCK)[None]
    valid = in_window[None] & has_prev
    logits = jnp.where(valid[None, :, None, None], logits, -1e30)
    s = sink.astype(jnp.float32).reshape(ATTN_KV_HEADS, ATTN_GROUP)[None, None, :, :, None, None]
    m = jnp.maximum(jnp.max(logits, axis=-1, keepdims=True), s)
    p = jnp.exp(logits - m)
    probs = p / (jnp.sum(p, axis=-1, keepdims=True) + jnp.exp(s - m))
    out = jnp.einsum('bnhgqk,bnkhd->bnqhgd', probs.astype(v.dtype), vband)
    return out.reshape(B, S, ATTN_Q)


def conformer_conv(u, conv_w, conv_b, ln_g, ln_b):
    a, gate = jnp.split(u, 2, axis=-1)
    y = a * jax.nn.sigmoid(gate)
    y = lax.conv_general_dilated(y, conv_w[:, None, :].astype(y.dtype), window_strides=(1,),
                                 padding=[(CONV_WIDTH - 1, 0)],
                                 dimension_numbers=('NWC', 'WIO', 'NWC'),
                                 feature_group_count=CONV_CH)
    y = y + conv_b
    return jax.nn.silu(layer_norm(y, ln_g, ln_b))


def gla_chunked(q, k, v, gk):
    B, S, H, DK = q.shape
    DV = v.shape[-1]
    nc = S // GLA_CHUNK

    def chunks(t):
        return t.astype(jnp.float32).reshape(B, nc, GLA_CHUNK, H, t.shape[-1]).transpose(0, 1, 3, 2, 4)

    qc, kc, vc, gc = chunks(q), chunks(k), chunks(v), chunks(gk)
    b = jnp.cumsum(gc, axis=3)
    b_last = b[:, :, :, -1:, :]
    q_e = qc * (DK ** -0.5) * jnp.exp(b)
    k_e = kc * jnp.exp(-b)
    k_tail = kc * jnp.exp(b_last - b)
    causal = jnp.tril(jnp.ones((GLA_CHUNK, GLA_CHUNK), dtype=bool))
    att = jnp.where(causal, jnp.einsum('bnhtd,bnhsd->bnhts', q_e, k_e), 0.0)
    o_intra = jnp.einsum('bnhts,bnhsv->bnhtv', att, vc)
    kv = jnp.einsum('bnhsd,bnhsv->bnhdv', k_tail, vc)
    decay = jnp.exp(b_last[:, :, :, 0, :])

    def step(state, inp):
        dec, kvc = inp
        return dec[..., None] * state + kvc, state

    _, starts = lax.scan(step, jnp.zeros((B, H, DK, DV), jnp.float32),
                         (jnp.moveaxis(decay, 1, 0), jnp.moveaxis(kv, 1, 0)))
    starts = jnp.moveaxis(starts, 0, 1)
    o = o_intra + jnp.einsum('bnhtd,bnhdv->bnhtv', q_e, starts)
    return o.transpose(0, 1, 3, 2, 4).reshape(B, S, H, DV).astype(v.dtype)


def hybrid_mixer(h, w_in, attn_sink, rel_bias, conv_w, conv_b, conv_ln_g, conv_ln_b,
                 gla_gate_w2, gla_gate_b, gla_norm_g, w_a_up, w_b_up, w_c_up, w_out):
    B, S, _ = h.shape
    proj = jnp.einsum('bsd,dn->bsn', h, w_in)
    qa, ka, va, conv_in, qc, kc, vc, gc, gate_lr, gate_logits = jnp.split(proj, SPLIT_IDX, axis=-1)
    ya = sliding_window_attention(qa.reshape(B, S, ATTN_KV_HEADS, ATTN_GROUP, ATTN_HEAD_DIM),
                                  ka.reshape(B, S, ATTN_KV_HEADS, ATTN_HEAD_DIM),
                                  va.reshape(B, S, ATTN_KV_HEADS, ATTN_HEAD_DIM), attn_sink, rel_bias)
    yb = conformer_conv(conv_in, conv_w, conv_b, conv_ln_g, conv_ln_b)
    gk = jax.nn.log_sigmoid((jnp.einsum('bsr,rk->bsk', gate_lr, gla_gate_w2) + gla_gate_b).astype(jnp.float32)) / GLA_GATE_NORM
    oc = gla_chunked(qc.reshape(B, S, GLA_HEADS, GLA_DK_HEAD), kc.reshape(B, S, GLA_HEADS, GLA_DK_HEAD),
                     vc.reshape(B, S, GLA_HEADS, GLA_DV_HEAD), gk.reshape(B, S, GLA_HEADS, GLA_DK_HEAD))
    oc = rms_norm(oc, gla_norm_g) * jax.nn.silu(gc.reshape(B, S, GLA_HEADS, GLA_DV_HEAD))
    yc = oc.reshape(B, S, GLA_V)
    ya_d = jnp.einsum('bsc,cd->bsd', ya, w_a_up)
    yb_d = jnp.einsum('bsc,cd->bsd', yb, w_b_up)
    yc_d = jnp.einsum('bsc,cd->bsd', yc, w_c_up)
    gates = jax.nn.sigmoid(gate_logits).reshape(B, S, N_BRANCH, D_MODEL)
    merged = gates[:, :, 0] * ya_d + gates[:, :, 1] * yb_d + gates[:, :, 2] * yc_d
    return jnp.einsum('bsd,de->bse', merged, w_out)


def setup_inputs(seed: int = 0) -> dict:
    key = jax.random.key(seed)
    ks = jax.random.split(key, 32)

    def nrm(k, shape, fan_in):
        return jax.random.normal(k, shape, jnp.float32) * (fan_in ** -0.5)

    def gain(k, shape):
        return 1.0 + 0.02 * jax.random.normal(k, shape, jnp.float32)

    def small(k, shape, s=0.02):
        return s * jax.random.normal(k, shape, jnp.float32)

    L, D, F = DEPTH, D_MODEL, D_FF
    return {
        'x': jax.random.normal(ks[0], (BATCH, SEQ, D), jnp.float32),
        'rel_bias': small(ks[1], (REL_BUCKETS, ATTN_HEADS), 0.5),
        'ffn1_pre_g': gain(ks[2], (L, D)),
        'ffn1_post_g': gain(ks[3], (L, D)),
        'ffn1_w_gate': nrm(ks[4], (L, D, F), D),
        'ffn1_w_up': nrm(ks[5], (L, D, F), D),
        'ffn1_w_down': nrm(ks[6], (L, F, D), F),
        'mix_pre_g': gain(ks[7], (L, D)),
        'mix_post_g': gain(ks[8], (L, D)),
        'w_in': nrm(ks[9], (L, D, D_IN), D),
        'attn_sink': jax.random.normal(ks[10], (L, ATTN_HEADS), jnp.float32),
        'conv_w': nrm(ks[11], (L, CONV_WIDTH, CONV_CH), CONV_WIDTH),
        'conv_b': small(ks[12], (L, CONV_CH)),
        'conv_ln_g': gain(ks[13], (L, CONV_CH)),
        'conv_ln_b': small(ks[14], (L, CONV_CH)),
        'gla_gate_w2': nrm(ks[15], (L, GLA_GATE_RANK, GLA_K), GLA_GATE_RANK),
        'gla_gate_b': small(ks[16], (L, GLA_K), 0.1),
        'gla_norm_g': gain(ks[17], (L, GLA_DV_HEAD)),
        'w_a_up': nrm(ks[18], (L, ATTN_Q, D), ATTN_Q),
        'w_b_up': nrm(ks[19], (L, CONV_CH, D), CONV_CH),
        'w_c_up': nrm(ks[20], (L, GLA_V, D), GLA_V),
        'w_out': nrm(ks[21], (L, D, D), D),
        'ffn2_pre_g': gain(ks[22], (L, D)),
        'ffn2_post_g': gain(ks[23], (L, D)),
        'ffn2_w_gate': nrm(ks[24], (L, D, F), D),
        'ffn2_w_up': nrm(ks[25], (L, D, F), D),
        'ffn2_w_down': nrm(ks[26], (L, F, D), F),
    }


def reference(x, rel_bias, ffn1_pre_g, ffn1_post_g, ffn1_w_gate, ffn1_w_up, ffn1_w_down,
              mix_pre_g, mix_post_g, w_in, attn_sink, conv_w, conv_b, conv_ln_g, conv_ln_b,
              gla_gate_w2, gla_gate_b, gla_norm_g, w_a_up, w_b_up, w_c_up, w_out,
              ffn2_pre_g, ffn2_post_g, ffn2_w_gate, ffn2_w_up, ffn2_w_down):
    for l in range(DEPTH):
        f1 = swiglu_ffn(rms_norm(x, ffn1_pre_g[l]), ffn1_w_gate[l], ffn1_w_up[l], ffn1_w_down[l])
        x = x + 0.5 * rms_norm(f1, ffn1_post_g[l])
        m = hybrid_mixer(rms_norm(x, mix_pre_g[l]), w_in[l], attn_sink[l], rel_bias,
                         conv_w[l], conv_b[l], conv_ln_g[l], conv_ln_b[l],
                         gla_gate_w2[l], gla_gate_b[l], gla_norm_g[l],
                         w_a_up[l], w_b_up[l], w_c_up[l], w_out[l])
        x = x + rms_norm(m, mix_post_g[l])
        f2 = swiglu_ffn(rms_norm(x, ffn2_pre_g[l]), ffn2_w_gate[l], ffn2_w_up[l], ffn2_w_down[l])
        x = x + 0.5 * rms_norm(f2, ffn2_post_g[l])
    return x
```

```python
import math
import numpy as np
import concourse.bass as bass
import concourse.mybir as mybir
from concourse.bass_utils import run_bass_kernel_spmd

F32 = mybir.dt.float32
BF16 = mybir.dt.bfloat16
AF = mybir.ActivationFunctionType
ALU = mybir.AluOpType

D = 2048
F = 5632
S = 4096
B = 8
L = 2
T = 512
KC = D // 128
FC = F // 128
EPS = 1e-6
NSLOT = 8
NBMAX = 439
GRAN = 256


class Sched:
    def __init__(self):
        self.ops = []

    @staticmethod
    def grans(ap):
        if ap is None:
            return ()
        sp = str(ap.space)
        if "SB" not in sp and "PSUM" not in sp.upper():
            return ()
        esz = 2 if ap.dtype == BF16 else 4
        a = ap.ap
        pstep = a[0][0]
        off = ap.offset
        fstart = off % pstep if pstep > 0 else off
        ext = 1
        for st, cnt in a[1:]:
            ext += (cnt - 1) * abs(st)
        b0 = fstart * esz
        b1 = (fstart + ext) * esz
        nm = ap.tensor.name
        gr = GRAN if "SB" in sp else 2048
        return [(nm, g) for g in range(b0 // gr, (b1 - 1) // gr + 1)]

    def add(self, eng, fn, reads=(), writes=(), dma=None):
        r = []
        for ap in reads:
            r.extend(self.grans(ap))
        w = []
        for ap in writes:
            w.extend(self.grans(ap))
        self.ops.append((eng, fn, r, w, dma))
        return len(self.ops) - 1

    def resolve(self):
        ops = self.ops
        n = len(ops)
        last_w = {}
        readers = {}
        deps = [None] * n
        chan = [None] * n
        for i, (eng, fn, r, w, dma) in enumerate(ops):
            ch = ("dma:" + dma) if dma else eng
            chan[i] = ch
            d = set()
            for g in r:
                x = last_w.get(g)
                if x is not None:
                    d.add(x)
            for g in w:
                x = last_w.get(g)
                if x is not None:
                    d.add(x)
                rr = readers.get(g)
                if rr:
                    d.update(rr.values())
            d.discard(i)
            for g in r:
                readers.setdefault(g, {})[ch] = i
            for g in w:
                last_w[g] = i
                readers[g] = {}
            if eng == "pe" and not dma:
                d = {j for j in d if chan[j] != "pe"}
            deps[i] = d
        signal = [False] * n
        for i in range(n):
            for j in deps[i]:
                signal[j] = True
        val = [0] * n
        cnt = {}
        for i in range(n):
            ch = chan[i]
            if ch.startswith("dma:"):
                cnt[ch] = cnt.get(ch, 0) + 16
                val[i] = cnt[ch]
            elif signal[i]:
                cnt[ch] = cnt.get(ch, 0) + 1
                val[i] = cnt[ch]
        seen = {}
        waits = [None] * n
        for i in range(n):
            eng = ops[i][0]
            sd = seen.setdefault(eng, {})
            need = {}
            for j in deps[i]:
                ch = chan[j]
                if val[j] > need.get(ch, 0):
                    need[ch] = val[j]
            wl = []
            for ch, v in need.items():
                if sd.get(ch, 0) < v:
                    sd[ch] = v
                    wl.append((ch, v))
            waits[i] = wl
        self.chan, self.val, self.signal, self.waits = chan, val, signal, waits
        return sorted(set(chan))


def _bucket_table():
    n = np.arange(128)
    max_exact = 16
    nf = np.maximum(n, 1).astype(np.float32)
    large = max_exact + (np.log(nf / np.float32(max_exact)).astype(np.float32)
                         / np.float32(math.log(128 / max_exact)) * np.float32(32 - max_exact)).astype(np.int32)
    large = np.minimum(large, 31)
    return np.where(n < max_exact, n, large)


def _const_tables():
    bt = _bucket_table()
    k = np.arange(128)[:, None]
    q = np.arange(128)[None, :]
    dist_prev = 128 + q - k
    dist_own = q - k
    valid = np.concatenate([dist_prev < 128, dist_own >= 0], axis=1)
    dist = np.concatenate([dist_prev, dist_own], axis=1)
    dist_c = np.clip(dist, 0, 127)
    bucket = bt[dist_c]
    oh = np.zeros((128, 32, 256), np.float32)
    for b in range(32):
        oh[:, b, :] = ((bucket == b) & valid)
    negm = np.where(valid, 0.0, -30000.0).astype(np.float32)
    s = np.arange(128)[:, None]
    t = np.arange(128)[None, :]
    gmask = ((s <= t) & ((s // 64) == (t // 64))).astype(np.float32)
    reset = np.ones(512, np.float32)
    reset[::64] = 0.0
    return oh, negm, gmask, reset


GAIN_NAMES = ["ffn1_pre_g", "ffn1_post_g", "mix_pre_g", "mix_post_g", "ffn2_pre_g", "ffn2_post_g"]
C_GAIN = 0
C_CONVW = C_GAIN + 6 * L * 16
C_CONVB = C_CONVW + L * 8 * 31
C_LNG = C_CONVB + L * 8
C_LNB = C_LNG + L * 8
C_GATEB = C_LNB + L * 8
C_GLAG = C_GATEB + L * 4
C_SINK = C_GLAG + L * 2
NCONST = C_SINK + L * 16
P_RB = 0
P_NEGM = 512
NPRO = 768
CB_ONES = 0
CB_IDENT = 128
CB_GMASK = 256
CB_RESET = 384
NCB = 896


def _host_consts(inp):
    c = np.zeros((128, NCONST), np.float32)
    p = np.arange(128)
    for gi, nm in enumerate(GAIN_NAMES):
        g = np.asarray(inp[nm], np.float32)
        for l in range(L):
            c[:, C_GAIN + (gi * L + l) * 16: C_GAIN + (gi * L + l) * 16 + 16] = g[l].reshape(16, 128).T
    cw = np.asarray(inp["conv_w"], np.float32)
    for l in range(L):
        c[:, C_CONVW + l * 248: C_CONVW + (l + 1) * 248] = cw[l].reshape(31, 8, 128).transpose(2, 1, 0).reshape(128, 248)
        for base, nm in ((C_CONVB, "conv_b"), (C_LNG, "conv_ln_g"), (C_LNB, "conv_ln_b")):
            c[:, base + l * 8: base + l * 8 + 8] = np.asarray(inp[nm], np.float32)[l].reshape(8, 128).T
        c[:, C_GATEB + l * 4: C_GATEB + l * 4 + 4] = np.asarray(inp["gla_gate_b"], np.float32)[l].reshape(4, 128).T
        c[:, C_GLAG + l * 2: C_GLAG + l * 2 + 2] = np.asarray(inp["gla_norm_g"], np.float32)[l].reshape(2, 128).T
        c[:, C_SINK + l * 16: C_SINK + l * 16 + 16] = np.asarray(inp["attn_sink"], np.float32)[l][None, :]
    oh, negm, gmask, reset = _const_tables()
    pro = np.zeros((128, NPRO), np.float32)
    pro[:, P_RB:P_RB + 512] = np.asarray(inp["rel_bias"], np.float32).reshape(1, 512)
    pro[:, P_NEGM:P_NEGM + 256] = negm
    cb = np.zeros((128, NCB), np.float32)
    cb[:, CB_ONES:CB_ONES + 128] = 1.0
    cb[:, CB_IDENT:CB_IDENT + 128] = np.eye(128, dtype=np.float32)
    cb[:, CB_GMASK:CB_GMASK + 128] = gmask
    cb[:, CB_RESET:CB_RESET + 512] = reset[None, :]
    w2 = np.zeros((16, L * 512), np.float32)
    for l in range(L):
        w2[:, l * 512:(l + 1) * 512] = np.asarray(inp["gla_gate_w2"], np.float32)[l]
    return c, pro, cb, oh.reshape(128, 32 * 256), w2


class Builder:
    def __init__(self, NT, layers, stop=None):
        self.NT = NT
        self.layers = layers
        self.stop = stop
        self.sc = Sched()
        self.wspecs = {}
        self.nc = bass.Bass("TRN2", target_bir_lowering=False)
        nc = self.nc
        self.xin = nc.dram_tensor("xin", [NT, 128, KC * T], F32, kind="ExternalInput").ap()
        self.ws = nc.dram_tensor("ws", [len(layers) * NBMAX, 128, 2048], F32, kind="ExternalInput").ap()
        self.dbg = nc.dram_tensor("dbg", [NT, 128, 32 * T], F32, kind="ExternalOutput").ap() if stop else None
        self.cst = nc.dram_tensor("cst", [128, NCONST], F32, kind="ExternalInput").ap()
        self.pro = nc.dram_tensor("pro", [128, NPRO], F32, kind="ExternalInput").ap()
        self.cb = nc.dram_tensor("cb", [128, NCB], F32, kind="ExternalInput").ap()
        self.ohd = nc.dram_tensor("ohd", [128, 32 * 256], F32, kind="ExternalInput").ap()
        self.w2d = nc.dram_tensor("w2d", [16, L * 512], F32, kind="ExternalInput").ap()
        self.out = nc.dram_tensor("out", [NT, 128, KC * T], F32, kind="ExternalOutput").ap()
        self.bank = 0
        self.bankset = None
        self.bankpos = {}
        self.slot = 0
        self.dmaid = 0

    def carve(self, off, dtype, shape, parts=128):
        n = 1
        for s in shape:
            n *= s
        esz = 2 if dtype == BF16 else 4
        assert off % 4 == 0
        ap = self.A[0:parts, off // 2: off // 2 + n * esz // 2]
        if dtype != BF16:
            ap = ap.bitcast(dtype)
        if len(shape) == 2:
            ap = ap.rearrange("p (a b) -> p a b", b=shape[1])
        elif len(shape) == 3:
            ap = ap.rearrange("p (a b c) -> p a b c", b=shape[1], c=shape[2])
        return ap

    def mm(self, out, lhsT, rhs, start, stop):
        self.sc.add("pe", lambda e: e.matmul(out, lhsT=lhsT, rhs=rhs, start=start, stop=stop),
                    reads=(lhsT, rhs), writes=(out,))

    def tr(self, out, in_, ident):
        self.sc.add("pe", lambda e: e.transpose(out, in_, ident), reads=(in_, ident), writes=(out,))

    def act(self, out, in_, func, bias=None, scale=None):
        kw = {}
        rd = [in_]
        if bias is not None:
            kw["bias"] = bias
            if not isinstance(bias, (int, float)):
                rd.append(bias)
        if scale is not None:
            kw["scale"] = scale
            if not isinstance(scale, (int, float)):
                rd.append(scale)
        self.sc.add("act", lambda e: e.activation(out=out, in_=in_, func=func, **kw), reads=rd, writes=(out,))

    def tt(self, out, in0, in1, op, eng="dve"):
        self.sc.add(eng, lambda e: e.tensor_tensor(out=out, in0=in0, in1=in1, op=op), reads=(in0, in1), writes=(out,))

    def ts(self, out, in0, s1, s2, op0, op1=None, eng="dve"):
        rd = [in0] + [s for s in (s1, s2) if s is not None and not isinstance(s, (int, float))]
        if op1 is None:
            self.sc.add(eng, lambda e: e.tensor_scalar(out=out, in0=in0, scalar1=s1, scalar2=None, op0=op0),
                        reads=rd, writes=(out,))
        else:
            self.sc.add(eng, lambda e: e.tensor_scalar(out=out, in0=in0, scalar1=s1, scalar2=s2, op0=op0, op1=op1),
                        reads=rd, writes=(out,))

    def stt(self, out, in0, scalar, in1, op0, op1):
        rd = [in0, in1] + ([] if isinstance(scalar, (int, float)) else [scalar])
        self.sc.add("dve", lambda e: e.scalar_tensor_tensor(out=out, in0=in0, scalar=scalar, in1=in1, op0=op0, op1=op1),
                    reads=rd, writes=(out,))

    def cp(self, out, in_, eng="dve"):
        if eng == "act":
            self.sc.add("act", lambda e: e.copy(out=out, in_=in_), reads=(in_,), writes=(out,))
        else:
            self.sc.add(eng, lambda e: e.tensor_copy(out=out, in_=in_), reads=(in_,), writes=(out,))

    def recip(self, out, in_):
        self.sc.add("dve", lambda e: e.reciprocal(out=out, in_=in_), reads=(in_,), writes=(out,))

    def memset(self, ap, v, eng="dve"):
        self.sc.add(eng, lambda e: e.memset(ap, v), writes=(ap,))

    def dma(self, eng, out, in_, key=None):
        if key is None:
            self.dmaid += 1
            key = "u%d" % self.dmaid
        return self.sc.add(eng, lambda e: e.dma_start(out=out, in_=in_), reads=(in_,), writes=(out,), dma=key)

    def nb(self):
        if self.bankset is not None:
            bs, key = self.bankset
            i = self.bankpos.get(key, 0)
            self.bankpos[key] = (i + 1) % len(bs)
            return bs[i]
        b = self.bank
        self.bank = (self.bank + 1) % 8
        return b

    def wnext(self, l, t, P, E, spec):
        if t == 0 or l not in self.wspecs or len(self.wspecs[l]) <= self.wi:
            self.wspecs.setdefault(l, []).append((P, E, spec))
        bi = self.wi
        self.wi += 1
        s = self.slot
        self.slot = (self.slot + 1) % NSLOT
        dst = self.WB[0:P, s, 0:E]
        self.dma("pool", dst, self.ws[self.layers.index(l) * NBMAX + bi, 0:P, 0:E], key="wb%d" % s)
        return dst

    def colsum_bcast(self, srcs, ps):
        n = len(srcs)
        for i, s_ in enumerate(srcs):
            self.mm(ps, self.ONES, s_, i == 0, i == n - 1)

    def rstd_from(self, ps, dst, inv_n):
        self.act(dst, ps, AF.Ln, bias=self.EPSC, scale=inv_n)
        self.act(dst, dst, AF.Exp, scale=-0.5)

    def gcol(self, gi, l, c):
        k = C_GAIN + (gi * L + l) * 16 + c
        return self.CONST[:, k:k + 1]

    def prenorm(self, l, gi):
        ps = self.PS[:, self.nb(), :]
        for c in range(KC):
            sq = self.SQ[:, c % 2, :]
            self.act(sq, self.X[:, c, :], AF.Square)
            self.mm(ps, self.ONES, sq, c == 0, c == KC - 1)
        self.rstd_from(ps, self.RSTD, 1.0 / D)
        for c in range(KC):
            self.stt(self.H[:, c, :], self.X[:, c, :], self.gcol(gi, l, c), self.RSTD, ALU.mult, ALU.mult)

    def postnorm_residual(self, l, gi, PN, coef, sqdone_ps):
        self.rstd_from(sqdone_ps, self.RSTD, 1.0 / D)
        for c in range(KC):
            tmp = self.TMP[:, c % 2, :]
            self.stt(tmp, PN[:, c, :], self.gcol(gi, l, c), self.RSTD, ALU.mult, ALU.mult)
            self.stt(self.X[:, c, :], tmp, float(coef), self.X[:, c, :], ALU.mult, ALU.add)

    def ffn(self, l, t, which):
        gi_pre, gi_post = (0, 1) if which == 1 else (4, 5)
        n_gate, n_up, n_down = (("ffn1_w_gate", "ffn1_w_up", "ffn1_w_down") if which == 1
                                else ("ffn2_w_gate", "ffn2_w_up", "ffn2_w_down"))
        self.prenorm(l, gi_pre)
        ACTB = self.carve(self.o_big, BF16, [FC, T])
        for f in range(FC):
            def spec_g(inp, l=l, f=f, nm=n_gate):
                return np.asarray(inp[nm][l][:, f * 128:(f + 1) * 128]).reshape(16, 128, 128).transpose(1, 0, 2).reshape(128, 2048)
            def spec_u(inp, l=l, f=f, nm=n_up):
                return np.asarray(inp[nm][l][:, f * 128:(f + 1) * 128]).reshape(16, 128, 128).transpose(1, 0, 2).reshape(128, 2048)
            wg = self.wnext(l, t, 128, 2048, spec_g)
            pg = self.PS[:, self.nb(), :]
            for k in range(KC):
                self.mm(pg, wg[:, k * 128:(k + 1) * 128], self.H[:, k, :], k == 0, k == KC - 1)
            wu = self.wnext(l, t, 128, 2048, spec_u)
            pu = self.PS[:, self.nb(), :]
            for k in range(KC):
                self.mm(pu, wu[:, k * 128:(k + 1) * 128], self.H[:, k, :], k == 0, k == KC - 1)
            sg = self.TMP[:, f % 2, :]
            self.act(sg, pg, AF.Silu)
            self.tt(ACTB[:, f, :], sg, pu, ALU.mult)
        PN = self.carve(self.o_h, F32, [KC, T])
        pss = self.PS[:, self.nb(), :]
        for c in range(KC):
            po = self.fresh_bank([pss])
            k0 = 0
            for part, nk in enumerate((16, 16, 12)):
                def spec_d(inp, l=l, c=c, k0=k0, nk=nk, nm=n_down):
                    w = np.asarray(inp[nm][l][k0 * 128:(k0 + nk) * 128, c * 128:(c + 1) * 128])
                    return w.reshape(nk, 128, 128).transpose(1, 0, 2).reshape(128, nk * 128)
                wd = self.wnext(l, t, 128, nk * 128, spec_d)
                for k in range(nk):
                    self.mm(po, wd[:, k * 128:(k + 1) * 128], ACTB[:, k0 + k, :], (k0 + k) == 0, (k0 + k) == FC - 1)
                k0 += nk
            self.cp(PN[:, c, :], po, eng="act")
            sq = self.SQ[:, c % 2, :]
            self.act(sq, po, AF.Square)
            if c > 0:
                self.mm(pss, self.ONES, self.SQ[:, (c - 1) % 2, :], c == 1, False)
        self.mm(pss, self.ONES, self.SQ[:, (KC - 1) % 2, :], False, True)
        self.postnorm_residual(l, gi_post, PN, 0.5, pss)

    def _bank_of(self, ps_ap):
        return (ps_ap.offset % ps_ap.ap[0][0]) // 512

    def fresh_bank(self, avoid):
        av = {self._bank_of(a) for a in avoid}
        while True:
            b = self.nb()
            if b not in av:
                return self.PS[:, b, :]

    def mixer(self, l, t):
        first = (t == 0)
        self.prenorm(l, 2)
        H = self.H
        o_ws = self.o_extra
        YA = self.carve(o_ws, BF16, [8, T])
        YB = self.carve(o_ws + 16384, BF16, [8, T])
        YC = self.carve(o_ws + 24576, BF16, [8, T])
        oW = o_ws + 32768
        WIN = "w_in"
        oQ, oK, oV, oCV, oQC, oKC, oVC, oGC, oLR, oGL = 0, 1024, 1280, 1536, 3584, 4096, 4608, 5632, 6656, 6672

        def colblock(col0, ncols=128):
            def spec(inp, l=l, col0=col0, ncols=ncols):
                w = np.asarray(inp[WIN][l][:, col0:col0 + ncols])
                return w.reshape(16, 128, ncols).transpose(1, 0, 2).reshape(128, 16 * ncols)
            return spec

        def proj_fm(col0, M=128, ncols=128, wsl=None):
            w = self.wnext(l, t, 128, 16 * ncols, colblock(col0, ncols))
            return w

        oAW = self.o_attw
        QA = self.carve(oAW, BF16, [2, T])
        KA = self.carve(oAW + 4096, BF16, [4, 640])
        VA = self.carve(oAW + 9216, BF16, [5, 256])
        E = self.carve(oAW + 11776, F32, [2, 256])
        P = self.carve(oAW + 13824, BF16, [2, 256])
        RD = self.carve(oAW + 14848, F32, [2, 128])
        for g in range(4):
            def spec_k(inp, l=l, g=g):
                w = np.asarray(inp[WIN][l][:, oK + g * 64: oK + (g + 1) * 64])
                w = np.concatenate([w, w], axis=1)
                return w.reshape(16, 128, 128).transpose(1, 0, 2).reshape(128, 2048)
            w = self.wnext(l, t, 128, 2048, spec_k)
            ps = self.PS[:, self.nb(), :]
            for k in range(KC):
                self.mm(ps, w[:, k * 128:(k + 1) * 128], H[:, k, :], k == 0, k == KC - 1)
            if not first:
                self.cp(KA[:, g, 0:128], self.KCARRY[:, l, g, :], eng="dve")
            self.cp(KA[:, g, 128:640], ps, eng="act")
            self.cp(self.KCARRY[:, l, g, :], KA[:, g, 512:640], eng="dve")
        vps = [self.PS[:, self.nb(), 0:256] for _ in range(4)]
        for half in range(2):
            def spec_v(inp, l=l, half=half):
                w = np.asarray(inp[WIN][l][half * 1024:(half + 1) * 1024, oV:oV + 256])
                return w.reshape(8, 128, 256).transpose(1, 0, 2).reshape(128, 2048)
            w = self.wnext(l, t, 128, 2048, spec_v)
            for tb in range(4):
                for k in range(8):
                    kk = half * 8 + k
                    self.mm(vps[tb], H[:, kk, tb * 128:(tb + 1) * 128], w[:, k * 256:(k + 1) * 256], kk == 0, kk == KC - 1)
        if not first:
            self.cp(VA[:, 0, :], self.VCARRY[:, l, :], eng="dve")
        for tb in range(4):
            self.cp(VA[:, 1 + tb, :], vps[tb], eng="act")
        self.cp(self.VCARRY[:, l, :], VA[:, 4, :], eng="dve")
        VCt = self.carve(o_ws + 8192, BF16, [4, 1024])
        GLR = self.carve(oW + 27136, BF16, [T], parts=16)
        def attn_gen():
            units = [(g, qb, hl) for g in range(4) for qb in range(4) for hl in range(4)]
            held = {}

            def qproj(g):
                for hp in range(2):
                    w = self.wnext(l, t, 128, 2048, colblock(oQ + (g * 4 + hp * 2) * 64))
                    ps = self.PS[:, self.nb(), :]
                    for k in range(KC):
                        self.mm(ps, w[:, k * 128:(k + 1) * 128], H[:, k, :], k == 0, k == KC - 1)
                    self.cp(QA[:, hp, :], ps, eng="act")

            def stage1(i):
                g, qb, hl = units[i]
                if qb == 0 and hl == 0:
                    qproj(g)
                noprev = first and qb == 0
                h = g * 4 + hl
                hp, hf = hl // 2, hl % 2
                p0, p1 = hf * 64, hf * 64 + 64
                i2 = i % 2
                sps = self.PS[:, self.nb(), 0:256]
                q_ = QA[p0:p1, hp, qb * 128:(qb + 1) * 128]
                if not noprev:
                    self.mm(sps[:, 0:128], KA[p0:p1, g, qb * 128:(qb + 1) * 128], q_, True, True)
                self.mm(sps[:, 128:256], KA[p0:p1, g, (qb + 1) * 128:(qb + 2) * 128], q_, True, True)
                c0 = 128 if noprev else 0
                self.act(E[:, i2, c0:256], sps[:, c0:256], AF.Exp, scale=0.125)
                self.tt(P[:, i2, c0:256], E[:, i2, c0:256], self.EB[:, h, c0:256], ALU.mult)

            def stage2(i):
                g, qb, hl = units[i]
                noprev = first and qb == 0
                h = g * 4 + hl
                hf = hl % 2
                p0, p1 = hf * 64, hf * 64 + 64
                i2 = i % 2
                ops_ = self.PS[p0:p1, self.nb(), 0:256]
                if not noprev:
                    self.mm(ops_[:, 0:128], VA[:, qb, g * 64:(g + 1) * 64], P[:, i2, 0:128], True, False)
                self.mm(ops_[:, 0:128], VA[:, qb + 1, g * 64:(g + 1) * 64], P[:, i2, 128:256], noprev, True)
                if not noprev:
                    self.mm(ops_[:, 128:256], self.ONES[:, 0:64], P[:, i2, 0:128], True, False)
                self.mm(ops_[:, 128:256], self.ONES[:, 0:64], P[:, i2, 128:256], noprev, True)
                rd = RD[p0:p1, i2, :]
                ks = C_SINK + l * 16 + h
                self.act(rd, ops_[:, 128:256], AF.Ln, bias=self.CONST[p0:p1, ks:ks + 1], scale=1.0)
                self.act(rd, rd, AF.Exp, scale=-1.0)
                self.tt(YA[p0:p1, h // 2, qb * 128:(qb + 1) * 128], ops_[:, 0:128], rd, ALU.mult)

            stage1(0)
            yield
            for i in range(len(units)):
                if i + 1 < len(units):
                    stage1(i + 1)
                stage2(i)
                yield

        CO = self.carve(oW, F32, [8, T])
        YBUF = self.carve(oW + 16384, BF16, [2, 544])
        SGB = self.carve(oW + 20736, F32, [2, T])
        NDG = 8
        DG = self.carve(oW + 24832, BF16, [NDG, 128])
        CCB = self.CCARRYB
        def conv_gen():
          for c in range(8):
              wa = self.wnext(l, t, 128, 2048, colblock(oCV + c * 128))
              pa = self.PS[:, self.nb(), :]
              for k in range(KC):
                  self.mm(pa, wa[:, k * 128:(k + 1) * 128], H[:, k, :], k == 0, k == KC - 1)
              wg = self.wnext(l, t, 128, 2048, colblock(oCV + 1024 + c * 128))
              pg = self.PS[:, self.nb(), :]
              for k in range(KC):
                  self.mm(pg, wg[:, k * 128:(k + 1) * 128], H[:, k, :], k == 0, k == KC - 1)
              yb = YBUF[:, c % 2, :]
              sg = SGB[:, c % 2, :]
              self.act(sg, pg, AF.Sigmoid)
              self.cp(yb[:, 0:30], CCB[:, l, c, :], eng="dve")
              self.tt(yb[:, 30:30 + T], sg, pa, ALU.mult)
              self.cp(CCB[:, l, c, :], yb[:, T:T + 30], eng="dve")
              yield
              kw = C_CONVW + l * 248 + c * 31
              kb = C_CONVB + l * 8 + c
              pc = self.PS[:, self.nb(), :]
              for j in range(31):
                  dg = DG[:, (c * 31 + j) % NDG, :]
                  self.ts(dg, self.IDENT, self.CONST[:, kw + j:kw + j + 1], None, ALU.mult)
                  self.mm(pc, dg, yb[:, j:j + T], j == 0, j == 30)
                  if j == 15:
                      yield
              self.act(CO[:, c, :], pc, AF.Identity, bias=self.CONST[:, kb:kb + 1], scale=1.0)
              yield

          for pas in range(2):
              vps = [self.PS[:, self.nb(), :] for _ in range(4)]
              for q4 in range(4):
                  def spec_vc(inp, l=l, pas=pas, q4=q4):
                      w = np.asarray(inp[WIN][l][q4 * 512:(q4 + 1) * 512, oVC + pas * 512: oVC + (pas + 1) * 512])
                      return w.reshape(4, 128, 512).transpose(1, 0, 2).reshape(128, 2048)
                  w = self.wnext(l, t, 128, 2048, spec_vc)
                  for tb in range(4):
                      for k in range(4):
                          kk = q4 * 4 + k
                          self.mm(vps[tb], H[:, kk, tb * 128:(tb + 1) * 128], w[:, k * 512:(k + 1) * 512], kk == 0, kk == KC - 1)
                  yield
              for tb in range(4):
                  self.cp(VCt[:, tb, pas * 512:(pas + 1) * 512], vps[tb], eng="act")
          w = self.wnext(l, t, 128, 256, colblock(oLR, 16))
          psl = self.PS[0:16, self.nb(), :]
          for k in range(KC):
              self.mm(psl, w[:, k * 16:(k + 1) * 16], H[:, k, :], k == 0, k == KC - 1)
          self.cp(GLR, psl, eng="act")
          yield

        def run_threads(ga, gb, ratio):
            da = db = False
            while not (da and db):
                if not da:
                    self.bankset = ([0, 1, 2, 3], "A")
                    for _ in range(ratio):
                        try:
                            next(ga)
                        except StopIteration:
                            da = True
                            break
                if not db:
                    self.bankset = ([4, 5, 6, 7], "B")
                    try:
                        next(gb)
                    except StopIteration:
                        db = True
            self.bankset = None
        if self.stop == "attn":
            for _ in attn_gen():
                pass
            return [YA]
        run_threads(attn_gen(), conv_gen(), 2)
        psm = self.PS[:, self.nb(), :]
        pss = self.PS[:, self.nb(), :]
        for c in range(8):
            cb_ = self.SQ[:, 0, :]
            sq = self.SQ[:, 1, :]
            self.cp(cb_, CO[:, c, :], eng="act")
            self.act(sq, CO[:, c, :], AF.Square)
            self.mm(psm, self.ONES, cb_, c == 0, c == 7)
            self.mm(pss, self.ONES, sq, c == 0, c == 7)
        MEAN = self.MEAN
        self.act(MEAN, psm, AF.Copy, scale=1.0 / 1024)
        msq = self.TMP[:, 0, :]
        self.tt(msq, MEAN, MEAN, ALU.mult)
        var = self.TMP[:, 1, :]
        self.stt(var, pss, 1.0 / 1024, msq, ALU.mult, ALU.subtract)
        self.act(self.RSTD, var, AF.Ln, bias=self.EPSC, scale=1.0)
        self.act(self.RSTD, self.RSTD, AF.Exp, scale=-0.5)
        for c in range(8):
            tmp = self.TMP[:, c % 2, :]
            self.tt(tmp, CO[:, c, :], MEAN, ALU.subtract)
            self.tt(tmp, tmp, self.RSTD, ALU.mult)
            kg = C_LNG + l * 8 + c
            kb = C_LNB + l * 8 + c
            self.act(YB[:, c, :], tmp, AF.Silu, bias=self.CONST[:, kb:kb + 1], scale=self.CONST[:, kg:kg + 1])
        if self.stop == "conv":
            return [YA, YB]

        SP = self.carve(oW + 8192, F32, [T])
        CS = self.carve(oW + 10240, F32, [T])
        EBm = self.carve(oW + 12288, F32, [T])
        QE = self.carve(oW + 14336, BF16, [T])
        KE = self.carve(oW + 15360, BF16, [T])
        KTL = self.carve(oW + 16384, BF16, [T])
        KT = self.carve(oW + 17408, BF16, [4, 128])
        ATT = self.carve(oW + 18432, BF16, [2, 128])
        O = self.carve(oW + 18944, F32, [2, T])
        SGC = self.carve(oW + 23040, F32, [2, T])
        for hd in range(4):
            psz = self.PS[:, self.nb(), :]
            self.mm(psz, self.W2[:, l * 512 + hd * 128: l * 512 + (hd + 1) * 128], GLR, True, True)
            kgb = C_GATEB + l * 4 + hd
            self.act(SP, psz, AF.Exp, bias=self.NEGB[:, l * 4 + hd: l * 4 + hd + 1], scale=-1.0)
            self.act(SP, SP, AF.Ln, bias=self.ONEC, scale=1.0)
            self.sc.add("dve", lambda e, CS=CS, SP=SP: e.tensor_tensor_scan(out=CS, data0=self.RESET, data1=SP, initial=0.0,
                                                                      op0=ALU.mult, op1=ALU.add),
                        reads=(self.RESET, SP), writes=(CS,))
            self.act(EBm, CS, AF.Exp, scale=-1.0 / 16)
            EN = SP
            self.act(EN, CS, AF.Exp, scale=1.0 / 16)
            wq = self.wnext(l, t, 128, 2048, colblock(oQC + hd * 128))
            pq = self.PS[:, self.nb(), :]
            for k in range(KC):
                self.mm(pq, wq[:, k * 128:(k + 1) * 128], H[:, k, :], k == 0, k == KC - 1)
            self.stt(QE, pq, float(128 ** -0.5), EBm, ALU.mult, ALU.mult)
            wk = self.wnext(l, t, 128, 2048, colblock(oKC + hd * 128))
            pk = self.PS[:, self.nb(), :]
            for k in range(KC):
                self.mm(pk, wk[:, k * 128:(k + 1) * 128], H[:, k, :], k == 0, k == KC - 1)
            self.tt(KE, pk, EN, ALU.mult)
            EB3 = EBm.rearrange("p (c j) -> p c j", j=64)
            dec = EB3[:, :, 63:64]
            self.tt(KTL.rearrange("p (c j) -> p c j", j=64), KE.rearrange("p (c j) -> p c j", j=64),
                    dec.to_broadcast([128, 8, 64]), ALU.mult)
            for dvc in range(2):
                wgc = self.wnext(l, t, 128, 2048, colblock(oGC + hd * 256 + dvc * 128))
                pgc = self.PS[:, self.nb(), :]
                for k in range(KC):
                    self.mm(pgc, wgc[:, k * 128:(k + 1) * 128], H[:, k, :], k == 0, k == KC - 1)
                self.act(SGC[:, dvc, :], pgc, AF.Silu)
            for tb in range(4):
                pt = self.PS[:, self.nb(), 0:64].bitcast(BF16)
                self.tr(pt, KTL[:, tb * 128:(tb + 1) * 128], self.IDENT)
                self.cp(KT[:, tb, :], pt, eng="act")
            ST = self.STATE[:, l, hd, :]
            for tb in range(4):
                pkv = [self.PS[:, self.nb(), 0:256] for _ in range(2)]
                for ch in range(2):
                    self.mm(pkv[ch], KT[ch * 64:(ch + 1) * 64, tb, :],
                            VCt[ch * 64:(ch + 1) * 64, tb, hd * 256:(hd + 1) * 256], True, True)
                pat = self.PS[:, self.nb(), 0:128]
                self.mm(pat, KE[:, tb * 128:(tb + 1) * 128], QE[:, tb * 128:(tb + 1) * 128], True, True)
                at = ATT[:, tb % 2, :]
                self.tt(at, pat, self.GMASK, ALU.mult)
                skipA = first and tb == 0
                if not skipA:
                    self.cp(self.SB[:, 0, :], ST, eng="act")
                ca = tb * 2
                self.stt(ST, ST, EB3[:, ca, 63:64], pkv[0], ALU.mult, ALU.add)
                self.cp(self.SB[:, 1, :], ST, eng="act")
                po = self.PS[:, self.nb(), 0:256]
                for dvc in range(2):
                    o_ = po[:, dvc * 128:(dvc + 1) * 128]
                    self.mm(o_, VCt[:, tb, hd * 256 + dvc * 128: hd * 256 + (dvc + 1) * 128], at, True, False)
                    if not skipA:
                        self.mm(o_[:, 0:64], self.SB[:, 0, dvc * 128:(dvc + 1) * 128], QE[:, tb * 128: tb * 128 + 64], False, False)
                    self.mm(o_[:, 64:128], self.SB[:, 1, dvc * 128:(dvc + 1) * 128], QE[:, tb * 128 + 64: tb * 128 + 128], False, True)
                self.stt(ST, ST, EB3[:, ca + 1, 63:64], pkv[1], ALU.mult, ALU.add)
                for dvc in range(2):
                    self.cp(O[:, dvc, tb * 128:(tb + 1) * 128], po[:, dvc * 128:(dvc + 1) * 128], eng="act")
            if self.stop == "glad" and hd == 0:
                GD = self.carve(self.o_big + 30000 - 30000 % 4, BF16, [8, T])
                GD = self.carve(self.o_x, F32, [16, T])
                self.cp(GD[:, 0, :], CS, eng="act")
                self.cp(GD[:, 1, :], EBm, eng="act")
                self.cp(GD[:, 2, :], QE, eng="act")
                self.cp(GD[:, 3, :], KE, eng="act")
                self.cp(GD[:, 4, :], KTL, eng="act")
                self.cp(GD[:, 5, :], O[:, 0, :], eng="act")
                self.cp(GD[:, 6, :], O[:, 1, :], eng="act")
                self.cp(GD[:, 7, :], SGC[:, 0, :], eng="act")
                self.cp(GD[:, 8, :], KT.rearrange("p a b -> p (a b)"), eng="act")
                self.cp(GD[:, 9, :], VCt[:, :, 0:128], eng="act")
                self.cp(GD[:, 10, :], SP, eng="act")
                self.dma("sp", self.dbg[t].rearrange("p (c s) -> p c s", s=T)[:, 0:16, :], self.X, key="dbg")
            pss = self.PS[:, self.nb(), :]
            for dvc in range(2):
                sq = self.SQ[:, dvc, :]
                self.act(sq, O[:, dvc, :], AF.Square)
                self.mm(pss, self.ONES, sq, dvc == 0, dvc == 1)
            self.rstd_from(pss, self.RSTD, 1.0 / 256)
            for dvc in range(2):
                tmp = self.TMP[:, dvc, :]
                kg = C_GLAG + l * 2 + dvc
                self.stt(tmp, O[:, dvc, :], self.CONST[:, kg:kg + 1], self.RSTD, ALU.mult, ALU.mult)
                self.tt(YC[:, hd * 2 + dvc, :], tmp, SGC[:, dvc, :], ALU.mult)
        if self.stop == "gla":
            return [YA, YB, YC]
        if self.stop == "glad":
            return [YA]

        MG = self.carve(oW, BF16, [KC, T])
        SGT = self.carve(oW + 16384, F32, [2, T])
        ACC = self.carve(oW + 20480, F32, [2, T])
        for oc in range(KC):
            acc = ACC[:, oc % 2, :]
            for br in range(3):
                wgl = self.wnext(l, t, 128, 2048, colblock(oGL + br * D + oc * 128))
                pg = self.PS[:, self.nb(), :]
                for k in range(KC):
                    self.mm(pg, wgl[:, k * 128:(k + 1) * 128], H[:, k, :], k == 0, k == KC - 1)
                pu = self.PS[:, self.nb(), :]
                nm = "w_a_up" if br == 0 else ("w_b_up" if br == 1 else "w_c_up")
                def spec_bc(inp, l=l, oc=oc, nm=nm):
                    w = np.asarray(inp[nm][l][:, oc * 128:(oc + 1) * 128])
                    return w.reshape(8, 128, 128).transpose(1, 0, 2).reshape(128, 1024)
                wu = self.wnext(l, t, 128, 1024, spec_bc)
                Y = (YA, YB, YC)[br]
                for k in range(8):
                    self.mm(pu, wu[:, k * 128:(k + 1) * 128], Y[:, k, :], k == 0, k == 7)
                sg = SGT[:, br % 2, :]
                self.act(sg, pg, AF.Sigmoid)
                if br == 0:
                    self.tt(acc, sg, pu, ALU.mult)
                elif br == 1:
                    self.tt(sg, sg, pu, ALU.mult)
                    self.tt(acc, acc, sg, ALU.add)
                else:
                    self.tt(sg, sg, pu, ALU.mult)
                    self.tt(MG[:, oc, :], acc, sg, ALU.add)
        M = self.carve(o_ws, F32, [KC, T])
        pss = self.PS[:, self.nb(), :]
        for c in range(KC):
            def spec_o(inp, l=l, c=c):
                w = np.asarray(inp["w_out"][l][:, c * 128:(c + 1) * 128])
                return w.reshape(16, 128, 128).transpose(1, 0, 2).reshape(128, 2048)
            w = self.wnext(l, t, 128, 2048, spec_o)
            po = self.fresh_bank([pss])
            for k in range(KC):
                self.mm(po, w[:, k * 128:(k + 1) * 128], MG[:, k, :], k == 0, k == KC - 1)
            self.cp(M[:, c, :], po, eng="act")
            sq = self.SQ[:, c % 2, :]
            self.act(sq, po, AF.Square)
            if c > 0:
                self.mm(pss, self.ONES, self.SQ[:, (c - 1) % 2, :], c == 1, False)
        self.mm(pss, self.ONES, self.SQ[:, (KC - 1) % 2, :], False, True)
        self.postnorm_residual(l, 3, M, 1.0, pss)
        return None

    def build(self):
        nc = self.nc
        NT = self.NT
        o = 0
        self.o_x = o; o += KC * T * 4
        self.o_h = o; o += KC * T * 2
        self.o_extra = o; o += 16384
        self.o_big = o; o += FC * T * 2
        o_wb = o; o += NSLOT * 4096
        self.o_attw = o; o += 16384
        o_eb = o; o += 16 * 256 * 2
        o_sq = o; o += 2 * T * 2
        o_rstd = o; o += T * 4
        o_mean = o; o += T * 4
        o_tmp = o; o += 2 * T * 4
        o_state = o; o += L * 4 * 256 * 4
        o_sb = o; o += 2 * 256 * 2
        o_kc = o; o += L * 4 * 128 * 2
        o_vc = o; o += L * 256 * 2
        o_cc = o; o += L * 8 * 30 * 4
        o_cb = o; o += NCB * 2
        o_const = o; o += ((NCONST + 16) * 4 + 3) // 4 * 4
        o_w2 = o; o += L * 512 * 2
        o_negb = o; o += 8 * 4
        total = o
        self.total_bytes = total
        with (
            nc.sbuf_tensor("A", [128, total // 2], BF16) as A,
            nc.psum_tensor("PS", [128, 8, 512], F32) as PS,
        ):
            self.A = A
            self.PS = PS
            self.X = self.carve(self.o_x, F32, [KC, T])
            self.H = self.carve(self.o_h, BF16, [KC, T])
            self.WB = self.carve(o_wb, BF16, [NSLOT, 2048])
            self.EB = self.carve(o_eb, BF16, [16, 256])
            self.SQ = self.carve(o_sq, BF16, [2, T])
            self.RSTD = self.carve(o_rstd, F32, [T])
            self.MEAN = self.carve(o_mean, F32, [T])
            self.TMP = self.carve(o_tmp, F32, [2, T])
            self.STATE = self.carve(o_state, F32, [L, 4, 256])
            self.SB = self.carve(o_sb, BF16, [2, 256])
            self.KCARRY = self.carve(o_kc, BF16, [L, 4, 128])
            self.VCARRY = self.carve(o_vc, BF16, [L, 256])
            self.CCARRY = self.carve(o_cc, F32, [L, 8, 30])
            self.CCARRYB = self.carve(o_cc, BF16, [L, 8, 30])
            CB = self.carve(o_cb, BF16, [NCB])
            self.ONES = CB[:, CB_ONES:CB_ONES + 128]
            self.IDENT = CB[:, CB_IDENT:CB_IDENT + 128]
            self.GMASK = CB[:, CB_GMASK:CB_GMASK + 128]
            self.RESET = CB[:, CB_RESET:CB_RESET + 512]
            CONSTF = self.carve(o_const, F32, [NCONST + 16])
            self.CONST = CONSTF
            self.EPSC = CONSTF[:, NCONST:NCONST + 1]
            self.ONEC = CONSTF[:, NCONST + 1:NCONST + 2]
            self.W2 = self.carve(o_w2, BF16, [L * 512], parts=16)
            self.NEGB = self.carve(o_negb, F32, [8])

            self.dma("sp", CONSTF[:, 0:NCONST], self.cst[:, :])
            self.dma("pool", CB, self.cb[:, :])
            self.dma("pool", self.W2, self.w2d[:, :])
            oW = self.o_extra + 32768
            OH = self.carve(self.o_extra, BF16, [32, 256])
            PRO = self.carve(self.o_extra + 16384, F32, [NPRO])
            ACCB = self.carve(self.o_extra + 16384 + NPRO * 4, F32, [2, 256])
            for q4 in range(4):
                self.dma("pool", OH[:, q4 * 8:(q4 + 1) * 8, :], self.ohd[:, q4 * 2048:(q4 + 1) * 2048].rearrange("p (a b) -> p a b", b=256))
            self.dma("sp", PRO, self.pro[:, :])
            self.memset(self.EPSC, EPS)
            self.memset(self.ONEC, 1.0)
            self.memset(self.STATE, 0.0)
            self.memset(self.CCARRY, 0.0)
            for h in range(16):
                acc = ACCB[:, h % 2, :]
                self.stt(acc, OH[:, 0, :], PRO[:, P_RB + h: P_RB + h + 1], PRO[:, P_NEGM:P_NEGM + 256], ALU.mult, ALU.add)
                for b in range(1, 32):
                    self.stt(acc, OH[:, b, :], PRO[:, P_RB + b * 16 + h: P_RB + b * 16 + h + 1], acc, ALU.mult, ALU.add)
                self.act(self.EB[:, h, :], acc, AF.Exp)
            sk = CONSTF[:, C_SINK:C_SINK + L * 16]
            self.act(sk, sk, AF.Exp)
            self.act(self.NEGB, CONSTF[:, C_GATEB:C_GATEB + 8], AF.Copy, scale=-1.0)

            dbg = None
            for t in range(NT):
                self.dma("sp", self.X, self.xin[t].rearrange("p (c s) -> p c s", s=T), key="xld")
                for l in self.layers:
                    self.wi = 0
                    if self.stop == "none":
                        continue
                    if self.stop not in ("attn", "conv", "gla", "glad"):
                        self.ffn(l, t, 1)
                    if self.stop == "ffn1":
                        continue
                    dbg = self.mixer(l, t)
                    if dbg is not None:
                        continue
                    self.ffn(l, t, 2)
                if dbg is not None and self.stop == "glad":
                    dbg = None
                if dbg is not None:
                    YA_ = dbg[0]
                    dv = self.dbg[t].rearrange("p (c s) -> p c s", s=T)
                    for c in range(YA_.shape[1]):
                        self.cp(self.X[0:YA_.shape[0], c, :], YA_[:, c, :], eng="act")
                    self.dma("sp", dv[:, 0:16, :], self.X, key="dbg")
                    if len(dbg) > 1:
                        for j, Y_ in enumerate(dbg[1:]):
                            for c in range(8):
                                self.cp(self.X[:, j * 8 + c, :], Y_[:, c, :], eng="act")
                        self.dma("sp", dv[:, 16:32, :], self.X, key="dbg")
                st = self.dma("sp", self.out[t].rearrange("p (c s) -> p c s", s=T), self.X, key="xst")
            self.sc.add("sp", None, reads=(self.X,), writes=(self.X,))

            self.emit()
        return nc

    def emit(self):
        nc = self.nc
        sc = self.sc
        chans = sc.resolve()
        ops = sc.ops
        engmap = {"pe": "tensor", "act": "scalar", "dve": "vector", "pool": "gpsimd", "sp": "sync"}
        per_eng = {k: [] for k in engmap}
        for i, op in enumerate(ops):
            per_eng[op[0]].append(i)
        import contextlib
        with contextlib.ExitStack() as es:
            sems = {}
            for ch in chans:
                sems[ch] = es.enter_context(nc.semaphore("s_" + ch.replace(":", "_")))
            block = es.enter_context(nc.Block())

            def make(engname):
                idxs = per_eng[engname]

                def body(e):
                    for i in idxs:
                        eng, fn, r, w, dma = ops[i]
                        for ch, v in sc.waits[i]:
                            e.wait_ge(sems[ch], v)
                        if fn is None:
                            continue
                        ins = fn(e)
                        if dma:
                            ins.then_inc(sems[sc.chan[i]], 16)
                        elif sc.signal[i]:
                            ins.then_inc(sems[sc.chan[i]], 1)
                return body

            for engname, attr in engmap.items():
                if per_eng[engname]:
                    getattr(block, attr)(make(engname))


_CACHE = {}


def _get_builder(NT, layers, stop=None):
    key = (NT, tuple(layers), stop)
    if key not in _CACHE:
        b = Builder(NT, list(layers), stop)
        b.build()
        _CACHE[key] = b
    return _CACHE[key]


def _pack_weights(b, inp):
    ws = np.zeros((len(b.layers) * NBMAX, 128, 2048), np.float32)
    for l, specs in b.wspecs.items():
        assert len(specs) <= NBMAX, len(specs)
        li = b.layers.index(l)
        for bi, (P, E, spec) in enumerate(specs):
            arr = spec(inp)
            ws[li * NBMAX + bi, 0:arr.shape[0], 0:arr.shape[1]] = arr
    return ws


def _run(inp, NT=S // T, layers=(0, 1), stop=None, ncores=B):
    b = _get_builder(NT, layers, stop)
    print('blocks per layer', {l: len(v) for l, v in b.wspecs.items()}, 'sbuf bytes', b.total_bytes, 'nops', len(b.sc.ops), flush=True)
    inp = {k: np.asarray(v) for k, v in inp.items()}
    ws = _pack_weights(b, inp)
    c, pro, cb, oh, w2 = _host_consts(inp)
    x = inp["x"]
    in_maps = []
    for core in range(ncores):
        xb = x[core, :NT * T, :]
        xl = np.ascontiguousarray(xb.reshape(NT, T, KC, 128).transpose(0, 3, 2, 1)).reshape(NT, 128, KC * T)
        in_maps.append({"xin": xl, "ws": ws, "cst": c, "pro": pro, "cb": cb, "ohd": oh, "w2d": w2})
    res = run_bass_kernel_spmd(b.nc, in_maps, core_ids=list(range(ncores)))
    if stop:
        return [np.asarray(res.results[core]["dbg"]) for core in range(ncores)], [np.asarray(res.results[core]["out"]) for core in range(ncores)]
    outs = []
    for core in range(ncores):
        o = np.asarray(res.results[core]["out"]).reshape(NT, 128, KC, T).transpose(0, 3, 2, 1).reshape(NT * T, D)
        outs.append(o)
    return np.stack(outs, 0).astype(np.float32)


def kernel(**inputs):
    return _run(inputs)
```

```python
import math
import numpy as np
import concourse.bass as bass
import concourse.mybir as mybir
from concourse.bass_utils import run_bass_kernel_spmd

F32 = mybir.dt.float32
BF16 = mybir.dt.bfloat16
AF = mybir.ActivationFunctionType
ALU = mybir.AluOpType

D = 2048
F = 5632
S = 4096
B = 8
L = 2
T = 512
KC = D // 128
FC = F // 128
EPS = 1e-6
NSLOT = 8
NBMAX = 439
GRAN = 256


class Sched:
    def __init__(self):
        self.ops = []

    @staticmethod
    def grans(ap):
        if ap is None:
            return ()
        sp = str(ap.space)
        if "SB" not in sp and "PSUM" not in sp.upper():
            return ()
        esz = 2 if ap.dtype == BF16 else 4
        a = ap.ap
        pstep = a[0][0]
        off = ap.offset
        fstart = off % pstep if pstep > 0 else off
        ext = 1
        for st, cnt in a[1:]:
            ext += (cnt - 1) * abs(st)
        b0 = fstart * esz
        b1 = (fstart + ext) * esz
        nm = ap.tensor.name
        gr = GRAN if "SB" in sp else 2048
        return [(nm, g) for g in range(b0 // gr, (b1 - 1) // gr + 1)]

    def add(self, eng, fn, reads=(), writes=(), dma=None):
        r = []
        for ap in reads:
            r.extend(self.grans(ap))
        w = []
        for ap in writes:
            w.extend(self.grans(ap))
        self.ops.append((eng, fn, r, w, dma))
        return len(self.ops) - 1

    def resolve(self):
        ops = self.ops
        n = len(ops)
        last_w = {}
        readers = {}
        deps = [None] * n
        chan = [None] * n
        for i, (eng, fn, r, w, dma) in enumerate(ops):
            ch = ("dma:" + dma) if dma else eng
            chan[i] = ch
            d = set()
            for g in r:
                x = last_w.get(g)
                if x is not None:
                    d.add(x)
            for g in w:
                x = last_w.get(g)
                if x is not None:
                    d.add(x)
                rr = readers.get(g)
                if rr:
                    d.update(rr.values())
            d.discard(i)
            for g in r:
                readers.setdefault(g, {})[ch] = i
            for g in w:
                last_w[g] = i
                readers[g] = {}
            if eng == "pe" and not dma:
                d = {j for j in d if chan[j] != "pe"}
            deps[i] = d
        signal = [False] * n
        for i in range(n):
            for j in deps[i]:
                signal[j] = True
        val = [0] * n
        cnt = {}
        for i in range(n):
            ch = chan[i]
            if ch.startswith("dma:"):
                cnt[ch] = cnt.get(ch, 0) + 16
                val[i] = cnt[ch]
            elif signal[i]:
                cnt[ch] = cnt.get(ch, 0) + 1
                val[i] = cnt[ch]
        seen = {}
        waits = [None] * n
        for i in range(n):
            eng = ops[i][0]
            sd = seen.setdefault(eng, {})
            need = {}
            for j in deps[i]:
                ch = chan[j]
                if val[j] > need.get(ch, 0):
                    need[ch] = val[j]
            wl = []
            for ch, v in need.items():
                if sd.get(ch, 0) < v:
                    sd[ch] = v
                    wl.append((ch, v))
            waits[i] = wl
        self.chan, self.val, self.signal, self.waits = chan, val, signal, waits
        return sorted(set(chan))


def _bucket_table():
    n = np.arange(128)
    max_exact = 16
    nf = np.maximum(n, 1).astype(np.float32)
    large = max_exact + (np.log(nf / np.float32(max_exact)).astype(np.float32)
                         / np.float32(math.log(128 / max_exact)) * np.float32(32 - max_exact)).astype(np.int32)
    large = np.minimum(large, 31)
    return np.where(n < max_exact, n, large)


def _const_tables():
    bt = _bucket_table()
    k = np.arange(128)[:, None]
    q = np.arange(128)[None, :]
    dist_prev = 128 + q - k
    dist_own = q - k
    valid = np.concatenate([dist_prev < 128, dist_own >= 0], axis=1)
    dist = np.concatenate([dist_prev, dist_own], axis=1)
    dist_c = np.clip(dist, 0, 127)
    bucket = bt[dist_c]
    oh = np.zeros((128, 32, 256), np.float32)
    for b in range(32):
        oh[:, b, :] = ((bucket == b) & valid)
    negm = np.where(valid, 0.0, -30000.0).astype(np.float32)
    s = np.arange(128)[:, None]
    t = np.arange(128)[None, :]
    gmask = ((s <= t) & ((s // 64) == (t // 64))).astype(np.float32)
    reset = np.ones(512, np.float32)
    reset[::64] = 0.0
    return oh, negm, gmask, reset


GAIN_NAMES = ["ffn1_pre_g", "ffn1_post_g", "mix_pre_g", "mix_post_g", "ffn2_pre_g", "ffn2_post_g"]
C_GAIN = 0
C_CONVW = C_GAIN + 6 * L * 16
C_CONVB = C_CONVW + L * 8 * 31
C_LNG = C_CONVB + L * 8
C_LNB = C_LNG + L * 8
C_GATEB = C_LNB + L * 8
C_GLAG = C_GATEB + L * 4
C_SINK = C_GLAG + L * 2
NCONST = C_SINK + L * 16
P_RB = 0
P_NEGM = 512
NPRO = 768
CB_ONES = 0
CB_IDENT = 128
CB_GMASK = 256
CB_RESET = 384
NCB = 896


def _host_consts(inp):
    c = np.zeros((128, NCONST), np.float32)
    p = np.arange(128)
    for gi, nm in enumerate(GAIN_NAMES):
        g = np.asarray(inp[nm], np.float32)
        for l in range(L):
            c[:, C_GAIN + (gi * L + l) * 16: C_GAIN + (gi * L + l) * 16 + 16] = g[l].reshape(16, 128).T
    cw = np.asarray(inp["conv_w"], np.float32)
    for l in range(L):
        c[:, C_CONVW + l * 248: C_CONVW + (l + 1) * 248] = cw[l].reshape(31, 8, 128).transpose(2, 1, 0).reshape(128, 248)
        for base, nm in ((C_CONVB, "conv_b"), (C_LNG, "conv_ln_g"), (C_LNB, "conv_ln_b")):
            c[:, base + l * 8: base + l * 8 + 8] = np.asarray(inp[nm], np.float32)[l].reshape(8, 128).T
        c[:, C_GATEB + l * 4: C_GATEB + l * 4 + 4] = np.asarray(inp["gla_gate_b"], np.float32)[l].reshape(4, 128).T
        c[:, C_GLAG + l * 2: C_GLAG + l * 2 + 2] = np.asarray(inp["gla_norm_g"], np.float32)[l].reshape(2, 128).T
        c[:, C_SINK + l * 16: C_SINK + l * 16 + 16] = np.asarray(inp["attn_sink"], np.float32)[l][None, :]
    oh, negm, gmask, reset = _const_tables()
    pro = np.zeros((128, NPRO), np.float32)
    pro[:, P_RB:P_RB + 512] = np.asarray(inp["rel_bias"], np.float32).reshape(1, 512)
    pro[:, P_NEGM:P_NEGM + 256] = negm
    cb = np.zeros((128, NCB), np.float32)
    cb[:, CB_ONES:CB_ONES + 128] = 1.0
    cb[:, CB_IDENT:CB_IDENT + 128] = np.eye(128, dtype=np.float32)
    cb[:, CB_GMASK:CB_GMASK + 128] = gmask
    cb[:, CB_RESET:CB_RESET + 512] = reset[None, :]
    w2 = np.zeros((16, L * 512), np.float32)
    for l in range(L):
        w2[:, l * 512:(l + 1) * 512] = np.asarray(inp["gla_gate_w2"], np.float32)[l]
    return c, pro, cb, oh.reshape(128, 32 * 256), w2


class Builder:
    def __init__(self, NT, layers, stop=None):
        self.NT = NT
        self.layers = layers
        self.stop = stop
        self.sc = Sched()
        self.wspecs = {}
        self.nc = bass.Bass("TRN2", target_bir_lowering=False)
        nc = self.nc
        self.xin = nc.dram_tensor("xin", [NT, 128, KC * T], F32, kind="ExternalInput").ap()
        self.ws = nc.dram_tensor("ws", [len(layers) * NBMAX, 128, 2048], F32, kind="ExternalInput").ap()
        self.dbg = nc.dram_tensor("dbg", [NT, 128, 32 * T], F32, kind="ExternalOutput").ap() if stop else None
        self.cst = nc.dram_tensor("cst", [128, NCONST], F32, kind="ExternalInput").ap()
        self.pro = nc.dram_tensor("pro", [128, NPRO], F32, kind="ExternalInput").ap()
        self.cb = nc.dram_tensor("cb", [128, NCB], F32, kind="ExternalInput").ap()
        self.ohd = nc.dram_tensor("ohd", [128, 32 * 256], F32, kind="ExternalInput").ap()
        self.w2d = nc.dram_tensor("w2d", [16, L * 512], F32, kind="ExternalInput").ap()
        self.out = nc.dram_tensor("out", [NT, 128, KC * T], F32, kind="ExternalOutput").ap()
        self.bank = 0
        self.bankset = None
        self.bankpos = {}
        self.slot = 0
        self.dmaid = 0

    def carve(self, off, dtype, shape, parts=128):
        n = 1
        for s in shape:
            n *= s
        esz = 2 if dtype == BF16 else 4
        assert off % 4 == 0
        ap = self.A[0:parts, off // 2: off // 2 + n * esz // 2]
        if dtype != BF16:
            ap = ap.bitcast(dtype)
        if len(shape) == 2:
            ap = ap.rearrange("p (a b) -> p a b", b=shape[1])
        elif len(shape) == 3:
            ap = ap.rearrange("p (a b c) -> p a b c", b=shape[1], c=shape[2])
        return ap

    def mm(self, out, lhsT, rhs, start, stop):
        self.sc.add("pe", lambda e: e.matmul(out, lhsT=lhsT, rhs=rhs, start=start, stop=stop),
                    reads=(lhsT, rhs), writes=(out,))

    def tr(self, out, in_, ident):
        self.sc.add("pe", lambda e: e.transpose(out, in_, ident), reads=(in_, ident), writes=(out,))

    def act(self, out, in_, func, bias=None, scale=None):
        kw = {}
        rd = [in_]
        if bias is not None:
            kw["bias"] = bias
            if not isinstance(bias, (int, float)):
                rd.append(bias)
        if scale is not None:
            kw["scale"] = scale
            if not isinstance(scale, (int, float)):
                rd.append(scale)
        self.sc.add("act", lambda e: e.activation(out=out, in_=in_, func=func, **kw), reads=rd, writes=(out,))

    def tt(self, out, in0, in1, op, eng="dve"):
        self.sc.add(eng, lambda e: e.tensor_tensor(out=out, in0=in0, in1=in1, op=op), reads=(in0, in1), writes=(out,))

    def ts(self, out, in0, s1, s2, op0, op1=None, eng="dve"):
        rd = [in0] + [s for s in (s1, s2) if s is not None and not isinstance(s, (int, float))]
        if op1 is None:
            self.sc.add(eng, lambda e: e.tensor_scalar(out=out, in0=in0, scalar1=s1, scalar2=None, op0=op0),
                        reads=rd, writes=(out,))
        else:
            self.sc.add(eng, lambda e: e.tensor_scalar(out=out, in0=in0, scalar1=s1, scalar2=s2, op0=op0, op1=op1),
                        reads=rd, writes=(out,))

    def stt(self, out, in0, scalar, in1, op0, op1):
        rd = [in0, in1] + ([] if isinstance(scalar, (int, float)) else [scalar])
        self.sc.add("dve", lambda e: e.scalar_tensor_tensor(out=out, in0=in0, scalar=scalar, in1=in1, op0=op0, op1=op1),
                    reads=rd, writes=(out,))

    def cp(self, out, in_, eng="dve"):
        if eng == "act":
            self.sc.add("act", lambda e: e.copy(out=out, in_=in_), reads=(in_,), writes=(out,))
        else:
            self.sc.add(eng, lambda e: e.tensor_copy(out=out, in_=in_), reads=(in_,), writes=(out,))

    def recip(self, out, in_):
        self.sc.add("dve", lambda e: e.reciprocal(out=out, in_=in_), reads=(in_,), writes=(out,))

    def memset(self, ap, v, eng="dve"):
        self.sc.add(eng, lambda e: e.memset(ap, v), writes=(ap,))

    def dma(self, eng, out, in_, key=None):
        if key is None:
            self.dmaid += 1
            key = "u%d" % self.dmaid
        return self.sc.add(eng, lambda e: e.dma_start(out=out, in_=in_), reads=(in_,), writes=(out,), dma=key)

    def nb(self):
        if self.bankset is not None:
            bs, key = self.bankset
            i = self.bankpos.get(key, 0)
            self.bankpos[key] = (i + 1) % len(bs)
            return bs[i]
        b = self.bank
        self.bank = (self.bank + 1) % 8
        return b

    def wnext(self, l, t, P, E, spec):
        if t == 0 or l not in self.wspecs or len(self.wspecs[l]) <= self.wi:
            self.wspecs.setdefault(l, []).append((P, E, spec))
        bi = self.wi
        self.wi += 1
        s = self.slot
        self.slot = (self.slot + 1) % NSLOT
        dst = self.WB[0:P, s, 0:E]
        self.dma("pool", dst, self.ws[self.layers.index(l) * NBMAX + bi, 0:P, 0:E], key="wb%d" % s)
        return dst

    def colsum_bcast(self, srcs, ps):
        n = len(srcs)
        for i, s_ in enumerate(srcs):
            self.mm(ps, self.ONES, s_, i == 0, i == n - 1)

    def rstd_from(self, ps, dst, inv_n):
        self.act(dst, ps, AF.Ln, bias=self.EPSC, scale=inv_n)
        self.act(dst, dst, AF.Exp, scale=-0.5)

    def gcol(self, gi, l, c):
        k = C_GAIN + (gi * L + l) * 16 + c
        return self.CONST[:, k:k + 1]

    def prenorm(self, l, gi):
        ps = self.PS[:, self.nb(), :]
        for c in range(KC):
            sq = self.SQ[:, c % 2, :]
            self.act(sq, self.X[:, c, :], AF.Square)
            self.mm(ps, self.ONES, sq, c == 0, c == KC - 1)
        self.rstd_from(ps, self.RSTD, 1.0 / D)
        for c in range(KC):
            self.stt(self.H[:, c, :], self.X[:, c, :], self.gcol(gi, l, c), self.RSTD, ALU.mult, ALU.mult)

    def postnorm_residual(self, l, gi, PN, coef, sqdone_ps):
        self.rstd_from(sqdone_ps, self.RSTD, 1.0 / D)
        for c in range(KC):
            tmp = self.TMP[:, c % 2, :]
            self.stt(tmp, PN[:, c, :], self.gcol(gi, l, c), self.RSTD, ALU.mult, ALU.mult)
            self.stt(self.X[:, c, :], tmp, float(coef), self.X[:, c, :], ALU.mult, ALU.add)

    def ffn(self, l, t, which):
        gi_pre, gi_post = (0, 1) if which == 1 else (4, 5)
        n_gate, n_up, n_down = (("ffn1_w_gate", "ffn1_w_up", "ffn1_w_down") if which == 1
                                else ("ffn2_w_gate", "ffn2_w_up", "ffn2_w_down"))
        self.prenorm(l, gi_pre)
        ACTB = self.carve(self.o_big, BF16, [FC, T])
        for f in range(FC):
            def spec_g(inp, l=l, f=f, nm=n_gate):
                return np.asarray(inp[nm][l][:, f * 128:(f + 1) * 128]).reshape(16, 128, 128).transpose(1, 0, 2).reshape(128, 2048)
            def spec_u(inp, l=l, f=f, nm=n_up):
                return np.asarray(inp[nm][l][:, f * 128:(f + 1) * 128]).reshape(16, 128, 128).transpose(1, 0, 2).reshape(128, 2048)
            wg = self.wnext(l, t, 128, 2048, spec_g)
            pg = self.PS[:, self.nb(), :]
            for k in range(KC):
                self.mm(pg, wg[:, k * 128:(k + 1) * 128], self.H[:, k, :], k == 0, k == KC - 1)
            wu = self.wnext(l, t, 128, 2048, spec_u)
            pu = self.PS[:, self.nb(), :]
            for k in range(KC):
                self.mm(pu, wu[:, k * 128:(k + 1) * 128], self.H[:, k, :], k == 0, k == KC - 1)
            sg = self.TMP[:, f % 2, :]
            self.act(sg, pg, AF.Silu)
            self.tt(ACTB[:, f, :], sg, pu, ALU.mult)
        PN = self.carve(self.o_h, F32, [KC, T])
        pss = self.PS[:, self.nb(), :]
        for c in range(KC):
            po = self.fresh_bank([pss])
            k0 = 0
            for part, nk in enumerate((16, 16, 12)):
                def spec_d(inp, l=l, c=c, k0=k0, nk=nk, nm=n_down):
                    w = np.asarray(inp[nm][l][k0 * 128:(k0 + nk) * 128, c * 128:(c + 1) * 128])
                    return w.reshape(nk, 128, 128).transpose(1, 0, 2).reshape(128, nk * 128)
                wd = self.wnext(l, t, 128, nk * 128, spec_d)
                for k in range(nk):
                    self.mm(po, wd[:, k * 128:(k + 1) * 128], ACTB[:, k0 + k, :], (k0 + k) == 0, (k0 + k) == FC - 1)
                k0 += nk
            self.cp(PN[:, c, :], po, eng="act")
            sq = self.SQ[:, c % 2, :]
            self.act(sq, po, AF.Square)
            if c > 0:
                self.mm(pss, self.ONES, self.SQ[:, (c - 1) % 2, :], c == 1, False)
        self.mm(pss, self.ONES, self.SQ[:, (KC - 1) % 2, :], False, True)
        self.postnorm_residual(l, gi_post, PN, 0.5, pss)

    def _bank_of(self, ps_ap):
        return (ps_ap.offset % ps_ap.ap[0][0]) // 512

    def fresh_bank(self, avoid):
        av = {self._bank_of(a) for a in avoid}
        while True:
            b = self.nb()
            if b not in av:
                return self.PS[:, b, :]

    def mixer(self, l, t):
        first = (t == 0)
        self.prenorm(l, 2)
        H = self.H
        o_ws = self.o_extra
        YA = self.carve(o_ws, BF16, [8, T])
        YB = self.carve(o_ws + 16384, BF16, [8, T])
        YC = self.carve(o_ws + 24576, BF16, [8, T])
        oW = o_ws + 32768
        WIN = "w_in"
        oQ, oK, oV, oCV, oQC, oKC, oVC, oGC, oLR, oGL = 0, 1024, 1280, 1536, 3584, 4096, 4608, 5632, 6656, 6672

        def colblock(col0, ncols=128):
            def spec(inp, l=l, col0=col0, ncols=ncols):
                w = np.asarray(inp[WIN][l][:, col0:col0 + ncols])
                return w.reshape(16, 128, ncols).transpose(1, 0, 2).reshape(128, 16 * ncols)
            return spec

        def proj_fm(col0, M=128, ncols=128, wsl=None):
            w = self.wnext(l, t, 128, 16 * ncols, colblock(col0, ncols))
            return w

        oAW = self.o_attw
        QA = self.carve(oAW, BF16, [2, T])
        KA = self.carve(oAW + 4096, BF16, [4, 640])
        VA = self.carve(oAW + 9216, BF16, [5, 256])
        E = self.carve(oAW + 11776, F32, [2, 256])
        P = self.carve(oAW + 13824, BF16, [2, 256])
        RD = self.carve(oAW + 14848, F32, [2, 128])
        for g in range(4):
            def spec_k(inp, l=l, g=g):
                w = np.asarray(inp[WIN][l][:, oK + g * 64: oK + (g + 1) * 64])
                w = np.concatenate([w, w], axis=1)
                return w.reshape(16, 128, 128).transpose(1, 0, 2).reshape(128, 2048)
            w = self.wnext(l, t, 128, 2048, spec_k)
            ps = self.PS[:, self.nb(), :]
            for k in range(KC):
                self.mm(ps, w[:, k * 128:(k + 1) * 128], H[:, k, :], k == 0, k == KC - 1)
            if not first:
                self.cp(KA[:, g, 0:128], self.KCARRY[:, l, g, :], eng="dve")
            self.cp(KA[:, g, 128:640], ps, eng="act")
            self.cp(self.KCARRY[:, l, g, :], KA[:, g, 512:640], eng="dve")
        vps = [self.PS[:, self.nb(), 0:256] for _ in range(4)]
        for half in range(2):
            def spec_v(inp, l=l, half=half):
                w = np.asarray(inp[WIN][l][half * 1024:(half + 1) * 1024, oV:oV + 256])
                return w.reshape(8, 128, 256).transpose(1, 0, 2).reshape(128, 2048)
            w = self.wnext(l, t, 128, 2048, spec_v)
            for tb in range(4):
                for k in range(8):
                    kk = half * 8 + k
                    self.mm(vps[tb], H[:, kk, tb * 128:(tb + 1) * 128], w[:, k * 256:(k + 1) * 256], kk == 0, kk == KC - 1)
        if not first:
            self.cp(VA[:, 0, :], self.VCARRY[:, l, :], eng="dve")
        for tb in range(4):
            self.cp(VA[:, 1 + tb, :], vps[tb], eng="act")
        self.cp(self.VCARRY[:, l, :], VA[:, 4, :], eng="dve")
        VCt = self.carve(o_ws + 8192, BF16, [4, 1024])
        GLR = self.carve(oW + 27136, BF16, [T], parts=16)
        def attn_gen():
            units = [(g, qb, hl) for g in range(4) for qb in range(4) for hl in range(4)]
            held = {}

            def qproj(g):
                for hp in range(2):
                    w = self.wnext(l, t, 128, 2048, colblock(oQ + (g * 4 + hp * 2) * 64))
                    ps = self.PS[:, self.nb(), :]
                    for k in range(KC):
                        self.mm(ps, w[:, k * 128:(k + 1) * 128], H[:, k, :], k == 0, k == KC - 1)
                    self.cp(QA[:, hp, :], ps, eng="act")

            def stage1(i):
                g, qb, hl = units[i]
                if qb == 0 and hl == 0:
                    qproj(g)
                noprev = first and qb == 0
                h = g * 4 + hl
                hp, hf = hl // 2, hl % 2
                p0, p1 = hf * 64, hf * 64 + 64
                i2 = i % 2
                sps = self.PS[:, self.nb(), 0:256]
                q_ = QA[p0:p1, hp, qb * 128:(qb + 1) * 128]
                if not noprev:
                    self.mm(sps[:, 0:128], KA[p0:p1, g, qb * 128:(qb + 1) * 128], q_, True, True)
                self.mm(sps[:, 128:256], KA[p0:p1, g, (qb + 1) * 128:(qb + 2) * 128], q_, True, True)
                c0 = 128 if noprev else 0
                self.act(E[:, i2, c0:256], sps[:, c0:256], AF.Exp, scale=0.125)
                self.tt(P[:, i2, c0:256], E[:, i2, c0:256], self.EB[:, h, c0:256], ALU.mult)

            def stage2(i):
                g, qb, hl = units[i]
                noprev = first and qb == 0
                h = g * 4 + hl
                hf = hl % 2
                p0, p1 = hf * 64, hf * 64 + 64
                i2 = i % 2
                ops_ = self.PS[p0:p1, self.nb(), 0:256]
                if not noprev:
                    self.mm(ops_[:, 0:128], VA[:, qb, g * 64:(g + 1) * 64], P[:, i2, 0:128], True, False)
                self.mm(ops_[:, 0:128], VA[:, qb + 1, g * 64:(g + 1) * 64], P[:, i2, 128:256], noprev, True)
                if not noprev:
                    self.mm(ops_[:, 128:256], self.ONES[:, 0:64], P[:, i2, 0:128], True, False)
                self.mm(ops_[:, 128:256], self.ONES[:, 0:64], P[:, i2, 128:256], noprev, True)
                rd = RD[p0:p1, i2, :]
                ks = C_SINK + l * 16 + h
                self.act(rd, ops_[:, 128:256], AF.Ln, bias=self.CONST[p0:p1, ks:ks + 1], scale=1.0)
                self.act(rd, rd, AF.Exp, scale=-1.0)
                self.tt(YA[p0:p1, h // 2, qb * 128:(qb + 1) * 128], ops_[:, 0:128], rd, ALU.mult)

            stage1(0)
            yield
            for i in range(len(units)):
                if i + 1 < len(units):
                    stage1(i + 1)
                stage2(i)
                yield

        CO = self.carve(oW, F32, [8, T])
        YBUF = self.carve(oW + 16384, BF16, [2, 544])
        SGB = self.carve(oW + 20736, F32, [2, T])
        NDG = 8
        DG = self.carve(oW + 24832, BF16, [NDG, 128])
        CCB = self.CCARRYB
        def conv_gen():
          for c in range(8):
              wa = self.wnext(l, t, 128, 2048, colblock(oCV + c * 128))
              pa = self.PS[:, self.nb(), :]
              for k in range(KC):
                  self.mm(pa, wa[:, k * 128:(k + 1) * 128], H[:, k, :], k == 0, k == KC - 1)
              wg = self.wnext(l, t, 128, 2048, colblock(oCV + 1024 + c * 128))
              pg = self.PS[:, self.nb(), :]
              for k in range(KC):
                  self.mm(pg, wg[:, k * 128:(k + 1) * 128], H[:, k, :], k == 0, k == KC - 1)
              yb = YBUF[:, c % 2, :]
              sg = SGB[:, c % 2, :]
              self.act(sg, pg, AF.Sigmoid)
              self.cp(yb[:, 0:30], CCB[:, l, c, :], eng="dve")
              self.tt(yb[:, 30:30 + T], sg, pa, ALU.mult)
              self.cp(CCB[:, l, c, :], yb[:, T:T + 30], eng="dve")
              yield
              kw = C_CONVW + l * 248 + c * 31
              kb = C_CONVB + l * 8 + c
              pc = self.PS[:, self.nb(), :]
              for j in range(31):
                  dg = DG[:, (c * 31 + j) % NDG, :]
                  self.ts(dg, self.IDENT, self.CONST[:, kw + j:kw + j + 1], None, ALU.mult)
                  self.mm(pc, dg, yb[:, j:j + T], j == 0, j == 30)
                  if j == 15:
                      yield
              self.act(CO[:, c, :], pc, AF.Identity, bias=self.CONST[:, kb:kb + 1], scale=1.0)
              yield

          for pas in range(2):
              vps = [self.PS[:, self.nb(), :] for _ in range(4)]
              for q4 in range(4):
                  def spec_vc(inp, l=l, pas=pas, q4=q4):
                      w = np.asarray(inp[WIN][l][q4 * 512:(q4 + 1) * 512, oVC + pas * 512: oVC + (pas + 1) * 512])
                      return w.reshape(4, 128, 512).transpose(1, 0, 2).reshape(128, 2048)
                  w = self.wnext(l, t, 128, 2048, spec_vc)
                  for tb in range(4):
                      for k in range(4):
                          kk = q4 * 4 + k
                          self.mm(vps[tb], H[:, kk, tb * 128:(tb + 1) * 128], w[:, k * 512:(k + 1) * 512], kk == 0, kk == KC - 1)
                  yield
              for tb in range(4):
                  self.cp(VCt[:, tb, pas * 512:(pas + 1) * 512], vps[tb], eng="act")
          w = self.wnext(l, t, 128, 256, colblock(oLR, 16))
          psl = self.PS[0:16, self.nb(), :]
          for k in range(KC):
              self.mm(psl, w[:, k * 16:(k + 1) * 16], H[:, k, :], k == 0, k == KC - 1)
          self.cp(GLR, psl, eng="act")
          yield

        def run_threads(ga, gb, ratio):
            da = db = False
            while not (da and db):
                if not da:
                    self.bankset = ([0, 1, 2, 3], "A")
                    for _ in range(ratio):
                        try:
                            next(ga)
                        except StopIteration:
                            da = True
                            break
                if not db:
                    self.bankset = ([4, 5, 6, 7], "B")
                    try:
                        next(gb)
                    except StopIteration:
                        db = True
            self.bankset = None
        if self.stop == "attn":
            for _ in attn_gen():
                pass
            return [YA]
        run_threads(attn_gen(), conv_gen(), 2)
        psm = self.PS[:, self.nb(), :]
        pss = self.PS[:, self.nb(), :]
        for c in range(8):
            cb_ = self.SQ[:, 0, :]
            sq = self.SQ[:, 1, :]
            self.cp(cb_, CO[:, c, :], eng="act")
            self.act(sq, CO[:, c, :], AF.Square)
            self.mm(psm, self.ONES, cb_, c == 0, c == 7)
            self.mm(pss, self.ONES, sq, c == 0, c == 7)
        MEAN = self.MEAN
        self.act(MEAN, psm, AF.Copy, scale=1.0 / 1024)
        msq = self.TMP[:, 0, :]
        self.tt(msq, MEAN, MEAN, ALU.mult)
        var = self.TMP[:, 1, :]
        self.stt(var, pss, 1.0 / 1024, msq, ALU.mult, ALU.subtract)
        self.act(self.RSTD, var, AF.Ln, bias=self.EPSC, scale=1.0)
        self.act(self.RSTD, self.RSTD, AF.Exp, scale=-0.5)
        for c in range(8):
            tmp = self.TMP[:, c % 2, :]
            self.tt(tmp, CO[:, c, :], MEAN, ALU.subtract)
            self.tt(tmp, tmp, self.RSTD, ALU.mult)
            kg = C_LNG + l * 8 + c
            kb = C_LNB + l * 8 + c
            self.act(YB[:, c, :], tmp, AF.Silu, bias=self.CONST[:, kb:kb + 1], scale=self.CONST[:, kg:kg + 1])
        if self.stop == "conv":
            return [YA, YB]

        SP = self.carve(oW + 8192, F32, [T])
        CS = self.carve(oW + 10240, F32, [T])
        EBm = self.carve(oW + 12288, F32, [T])
        QE = self.carve(oW + 14336, BF16, [T])
        KE = self.carve(oW + 15360, BF16, [T])
        KTL = self.carve(oW + 16384, BF16, [T])
        KT = self.carve(oW + 17408, BF16, [4, 128])
        ATT = self.carve(oW + 18432, BF16, [2, 128])
        O = self.carve(oW + 18944, F32, [2, T])
        SGC = self.carve(oW + 23040, F32, [2, T])
        for hd in range(4):
            psz = self.PS[:, self.nb(), :]
            self.mm(psz, self.W2[:, l * 512 + hd * 128: l * 512 + (hd + 1) * 128], GLR, True, True)
            kgb = C_GATEB + l * 4 + hd
            self.act(SP, psz, AF.Exp, bias=self.NEGB[:, l * 4 + hd: l * 4 + hd + 1], scale=-1.0)
            self.act(SP, SP, AF.Ln, bias=self.ONEC, scale=1.0)
            self.sc.add("dve", lambda e, CS=CS, SP=SP: e.tensor_tensor_scan(out=CS, data0=self.RESET, data1=SP, initial=0.0,
                                                                      op0=ALU.mult, op1=ALU.add),
                        reads=(self.RESET, SP), writes=(CS,))
            self.act(EBm, CS, AF.Exp, scale=-1.0 / 16)
            EN = SP
            self.act(EN, CS, AF.Exp, scale=1.0 / 16)
            wq = self.wnext(l, t, 128, 2048, colblock(oQC + hd * 128))
            pq = self.PS[:, self.nb(), :]
            for k in range(KC):
                self.mm(pq, wq[:, k * 128:(k + 1) * 128], H[:, k, :], k == 0, k == KC - 1)
            self.stt(QE, pq, float(128 ** -0.5), EBm, ALU.mult, ALU.mult)
            wk = self.wnext(l, t, 128, 2048, colblock(oKC + hd * 128))
            pk = self.PS[:, self.nb(), :]
            for k in range(KC):
                self.mm(pk, wk[:, k * 128:(k + 1) * 128], H[:, k, :], k == 0, k == KC - 1)
            self.tt(KE, pk, EN, ALU.mult)
            EB3 = EBm.rearrange("p (c j) -> p c j", j=64)
            dec = EB3[:, :, 63:64]
            self.tt(KTL.rearrange("p (c j) -> p c j", j=64), KE.rearrange("p (c j) -> p c j", j=64),
                    dec.to_broadcast([128, 8, 64]), ALU.mult)
            for dvc in range(2):
                wgc = self.wnext(l, t, 128, 2048, colblock(oGC + hd * 256 + dvc * 128))
                pgc = self.PS[:, self.nb(), :]
                for k in range(KC):
                    self.mm(pgc, wgc[:, k * 128:(k + 1) * 128], H[:, k, :], k == 0, k == KC - 1)
                self.act(SGC[:, dvc, :], pgc, AF.Silu)
            for tb in range(4):
                pt = self.PS[:, self.nb(), 0:64].bitcast(BF16)
                self.tr(pt, KTL[:, tb * 128:(tb + 1) * 128], self.IDENT)
                self.cp(KT[:, tb, :], pt, eng="act")
            ST = self.STATE[:, l, hd, :]
            pre = {}

            def gla_pre(tb):
                pkv = [self.PS[:, self.nb(), 0:256] for _ in range(2)]
                for ch in range(2):
                    self.mm(pkv[ch], KT[ch * 64:(ch + 1) * 64, tb, :],
                            VCt[ch * 64:(ch + 1) * 64, tb, hd * 256:(hd + 1) * 256], True, True)
                pat = self.PS[:, self.nb(), 0:128]
                self.mm(pat, KE[:, tb * 128:(tb + 1) * 128], QE[:, tb * 128:(tb + 1) * 128], True, True)
                at = ATT[:, tb % 2, :]
                self.tt(at, pat, self.GMASK, ALU.mult)
                pre[tb] = (pkv, at)

            gla_pre(0)
            for tb in range(4):
                if tb + 1 < 4:
                    gla_pre(tb + 1)
                pkv, at = pre[tb]
                SB0 = self.SB[:, (tb % 2) * 2, :]
                SB1 = self.SB[:, (tb % 2) * 2 + 1, :]
                skipA = first and tb == 0
                if not skipA:
                    self.cp(SB0, ST, eng="act")
                ca = tb * 2
                self.stt(ST, ST, EB3[:, ca, 63:64], pkv[0], ALU.mult, ALU.add)
                self.cp(SB1, ST, eng="act")
                po = self.PS[:, self.nb(), 0:256]
                for dvc in range(2):
                    o_ = po[:, dvc * 128:(dvc + 1) * 128]
                    self.mm(o_, VCt[:, tb, hd * 256 + dvc * 128: hd * 256 + (dvc + 1) * 128], at, True, False)
                    if not skipA:
                        self.mm(o_[:, 0:64], SB0[:, dvc * 128:(dvc + 1) * 128], QE[:, tb * 128: tb * 128 + 64], False, False)
                    self.mm(o_[:, 64:128], SB1[:, dvc * 128:(dvc + 1) * 128], QE[:, tb * 128 + 64: tb * 128 + 128], False, True)
                self.stt(ST, ST, EB3[:, ca + 1, 63:64], pkv[1], ALU.mult, ALU.add)
                for dvc in range(2):
                    self.cp(O[:, dvc, tb * 128:(tb + 1) * 128], po[:, dvc * 128:(dvc + 1) * 128], eng="act")
            if self.stop == "glad" and hd == 0:
                GD = self.carve(self.o_big + 30000 - 30000 % 4, BF16, [8, T])
                GD = self.carve(self.o_x, F32, [16, T])
                self.cp(GD[:, 0, :], CS, eng="act")
                self.cp(GD[:, 1, :], EBm, eng="act")
                self.cp(GD[:, 2, :], QE, eng="act")
                self.cp(GD[:, 3, :], KE, eng="act")
                self.cp(GD[:, 4, :], KTL, eng="act")
                self.cp(GD[:, 5, :], O[:, 0, :], eng="act")
                self.cp(GD[:, 6, :], O[:, 1, :], eng="act")
                self.cp(GD[:, 7, :], SGC[:, 0, :], eng="act")
                self.cp(GD[:, 8, :], KT.rearrange("p a b -> p (a b)"), eng="act")
                self.cp(GD[:, 9, :], VCt[:, :, 0:128], eng="act")
                self.cp(GD[:, 10, :], SP, eng="act")
                self.dma("sp", self.dbg[t].rearrange("p (c s) -> p c s", s=T)[:, 0:16, :], self.X, key="dbg")
            pss = self.PS[:, self.nb(), :]
            for dvc in range(2):
                sq = self.SQ[:, dvc, :]
                self.act(sq, O[:, dvc, :], AF.Square)
                self.mm(pss, self.ONES, sq, dvc == 0, dvc == 1)
            self.rstd_from(pss, self.RSTD, 1.0 / 256)
            for dvc in range(2):
                tmp = self.TMP[:, dvc, :]
                kg = C_GLAG + l * 2 + dvc
                self.stt(tmp, O[:, dvc, :], self.CONST[:, kg:kg + 1], self.RSTD, ALU.mult, ALU.mult)
                self.tt(YC[:, hd * 2 + dvc, :], tmp, SGC[:, dvc, :], ALU.mult)
        if self.stop == "gla":
            return [YA, YB, YC]
        if self.stop == "glad":
            return [YA]

        MG = self.carve(oW, BF16, [KC, T])
        SGT = self.carve(oW + 16384, F32, [2, T])
        ACC = self.carve(oW + 20480, F32, [2, T])
        for oc in range(KC):
            acc = ACC[:, oc % 2, :]
            for br in range(3):
                wgl = self.wnext(l, t, 128, 2048, colblock(oGL + br * D + oc * 128))
                pg = self.PS[:, self.nb(), :]
                for k in range(KC):
                    self.mm(pg, wgl[:, k * 128:(k + 1) * 128], H[:, k, :], k == 0, k == KC - 1)
                pu = self.PS[:, self.nb(), :]
                nm = "w_a_up" if br == 0 else ("w_b_up" if br == 1 else "w_c_up")
                def spec_bc(inp, l=l, oc=oc, nm=nm):
                    w = np.asarray(inp[nm][l][:, oc * 128:(oc + 1) * 128])
                    return w.reshape(8, 128, 128).transpose(1, 0, 2).reshape(128, 1024)
                wu = self.wnext(l, t, 128, 1024, spec_bc)
                Y = (YA, YB, YC)[br]
                for k in range(8):
                    self.mm(pu, wu[:, k * 128:(k + 1) * 128], Y[:, k, :], k == 0, k == 7)
                sg = SGT[:, br % 2, :]
                self.act(sg, pg, AF.Sigmoid)
                if br == 0:
                    self.tt(acc, sg, pu, ALU.mult)
                elif br == 1:
                    self.tt(sg, sg, pu, ALU.mult)
                    self.tt(acc, acc, sg, ALU.add)
                else:
                    self.tt(sg, sg, pu, ALU.mult)
                    self.tt(MG[:, oc, :], acc, sg, ALU.add)
        M = self.carve(o_ws, F32, [KC, T])
        pss = self.PS[:, self.nb(), :]
        for c in range(KC):
            def spec_o(inp, l=l, c=c):
                w = np.asarray(inp["w_out"][l][:, c * 128:(c + 1) * 128])
                return w.reshape(16, 128, 128).transpose(1, 0, 2).reshape(128, 2048)
            w = self.wnext(l, t, 128, 2048, spec_o)
            po = self.fresh_bank([pss])
            for k in range(KC):
                self.mm(po, w[:, k * 128:(k + 1) * 128], MG[:, k, :], k == 0, k == KC - 1)
            self.cp(M[:, c, :], po, eng="act")
            sq = self.SQ[:, c % 2, :]
            self.act(sq, po, AF.Square)
            if c > 0:
                self.mm(pss, self.ONES, self.SQ[:, (c - 1) % 2, :], c == 1, False)
        self.mm(pss, self.ONES, self.SQ[:, (KC - 1) % 2, :], False, True)
        self.postnorm_residual(l, 3, M, 1.0, pss)
        return None

    def build(self):
        nc = self.nc
        NT = self.NT
        o = 0
        self.o_x = o; o += KC * T * 4
        self.o_h = o; o += KC * T * 2
        self.o_extra = o; o += 16384
        self.o_big = o; o += FC * T * 2
        o_wb = o; o += NSLOT * 4096
        self.o_attw = o; o += 16384
        o_eb = o; o += 16 * 256 * 2
        o_sq = o; o += 2 * T * 2
        o_rstd = o; o += T * 4
        o_mean = o; o += T * 4
        o_tmp = o; o += 2 * T * 4
        o_state = o; o += L * 4 * 256 * 4
        o_sb = o; o += 4 * 256 * 2
        o_kc = o; o += L * 4 * 128 * 2
        o_vc = o; o += L * 256 * 2
        o_cc = o; o += L * 8 * 30 * 4
        o_cb = o; o += NCB * 2
        o_const = o; o += ((NCONST + 16) * 4 + 3) // 4 * 4
        o_w2 = o; o += L * 512 * 2
        o_negb = o; o += 8 * 4
        total = o
        self.total_bytes = total
        with (
            nc.sbuf_tensor("A", [128, total // 2], BF16) as A,
            nc.psum_tensor("PS", [128, 8, 512], F32) as PS,
        ):
            self.A = A
            self.PS = PS
            self.X = self.carve(self.o_x, F32, [KC, T])
            self.H = self.carve(self.o_h, BF16, [KC, T])
            self.WB = self.carve(o_wb, BF16, [NSLOT, 2048])
            self.EB = self.carve(o_eb, BF16, [16, 256])
            self.SQ = self.carve(o_sq, BF16, [2, T])
            self.RSTD = self.carve(o_rstd, F32, [T])
            self.MEAN = self.carve(o_mean, F32, [T])
            self.TMP = self.carve(o_tmp, F32, [2, T])
            self.STATE = self.carve(o_state, F32, [L, 4, 256])
            self.SB = self.carve(o_sb, BF16, [4, 256])
            self.KCARRY = self.carve(o_kc, BF16, [L, 4, 128])
            self.VCARRY = self.carve(o_vc, BF16, [L, 256])
            self.CCARRY = self.carve(o_cc, F32, [L, 8, 30])
            self.CCARRYB = self.carve(o_cc, BF16, [L, 8, 30])
            CB = self.carve(o_cb, BF16, [NCB])
            self.ONES = CB[:, CB_ONES:CB_ONES + 128]
            self.IDENT = CB[:, CB_IDENT:CB_IDENT + 128]
            self.GMASK = CB[:, CB_GMASK:CB_GMASK + 128]
            self.RESET = CB[:, CB_RESET:CB_RESET + 512]
            CONSTF = self.carve(o_const, F32, [NCONST + 16])
            self.CONST = CONSTF
            self.EPSC = CONSTF[:, NCONST:NCONST + 1]
            self.ONEC = CONSTF[:, NCONST + 1:NCONST + 2]
            self.W2 = self.carve(o_w2, BF16, [L * 512], parts=16)
            self.NEGB = self.carve(o_negb, F32, [8])

            self.dma("sp", CONSTF[:, 0:NCONST], self.cst[:, :])
            self.dma("pool", CB, self.cb[:, :])
            self.dma("pool", self.W2, self.w2d[:, :])
            oW = self.o_extra + 32768
            OH = self.carve(self.o_extra, BF16, [32, 256])
            PRO = self.carve(self.o_extra + 16384, F32, [NPRO])
            ACCB = self.carve(self.o_extra + 16384 + NPRO * 4, F32, [2, 256])
            for q4 in range(4):
                self.dma("pool", OH[:, q4 * 8:(q4 + 1) * 8, :], self.ohd[:, q4 * 2048:(q4 + 1) * 2048].rearrange("p (a b) -> p a b", b=256))
            self.dma("sp", PRO, self.pro[:, :])
            self.memset(self.EPSC, EPS)
            self.memset(self.ONEC, 1.0)
            self.memset(self.STATE, 0.0)
            self.memset(self.CCARRY, 0.0)
            for h in range(16):
                acc = ACCB[:, h % 2, :]
                self.stt(acc, OH[:, 0, :], PRO[:, P_RB + h: P_RB + h + 1], PRO[:, P_NEGM:P_NEGM + 256], ALU.mult, ALU.add)
                for b in range(1, 32):
                    self.stt(acc, OH[:, b, :], PRO[:, P_RB + b * 16 + h: P_RB + b * 16 + h + 1], acc, ALU.mult, ALU.add)
                self.act(self.EB[:, h, :], acc, AF.Exp)
            sk = CONSTF[:, C_SINK:C_SINK + L * 16]
            self.act(sk, sk, AF.Exp)
            self.act(self.NEGB, CONSTF[:, C_GATEB:C_GATEB + 8], AF.Copy, scale=-1.0)

            dbg = None
            for t in range(NT):
                self.dma("sp", self.X, self.xin[t].rearrange("p (c s) -> p c s", s=T), key="xld")
                for l in self.layers:
                    self.wi = 0
                    if self.stop == "none":
                        continue
                    if self.stop not in ("attn", "conv", "gla", "glad"):
                        self.ffn(l, t, 1)
                    if self.stop == "ffn1":
                        continue
                    dbg = self.mixer(l, t)
                    if dbg is not None:
                        continue
                    self.ffn(l, t, 2)
                if dbg is not None and self.stop == "glad":
                    dbg = None
                if dbg is not None:
                    YA_ = dbg[0]
                    dv = self.dbg[t].rearrange("p (c s) -> p c s", s=T)
                    for c in range(YA_.shape[1]):
                        self.cp(self.X[0:YA_.shape[0], c, :], YA_[:, c, :], eng="act")
                    self.dma("sp", dv[:, 0:16, :], self.X, key="dbg")
                    if len(dbg) > 1:
                        for j, Y_ in enumerate(dbg[1:]):
                            for c in range(8):
                                self.cp(self.X[:, j * 8 + c, :], Y_[:, c, :], eng="act")
                        self.dma("sp", dv[:, 16:32, :], self.X, key="dbg")
                st = self.dma("sp", self.out[t].rearrange("p (c s) -> p c s", s=T), self.X, key="xst")
            self.sc.add("sp", None, reads=(self.X,), writes=(self.X,))

            self.emit()
        return nc

    def emit(self):
        nc = self.nc
        sc = self.sc
        chans = sc.resolve()
        ops = sc.ops
        engmap = {"pe": "tensor", "act": "scalar", "dve": "vector", "pool": "gpsimd", "sp": "sync"}
        per_eng = {k: [] for k in engmap}
        for i, op in enumerate(ops):
            per_eng[op[0]].append(i)
        import contextlib
        with contextlib.ExitStack() as es:
            sems = {}
            for ch in chans:
                sems[ch] = es.enter_context(nc.semaphore("s_" + ch.replace(":", "_")))
            block = es.enter_context(nc.Block())

            def make(engname):
                idxs = per_eng[engname]

                def body(e):
                    for i in idxs:
                        eng, fn, r, w, dma = ops[i]
                        for ch, v in sc.waits[i]:
                            e.wait_ge(sems[ch], v)
                        if fn is None:
                            continue
                        ins = fn(e)
                        if dma:
                            ins.then_inc(sems[sc.chan[i]], 16)
                        elif sc.signal[i]:
                            ins.then_inc(sems[sc.chan[i]], 1)
                return body

            for engname, attr in engmap.items():
                if per_eng[engname]:
                    getattr(block, attr)(make(engname))


_CACHE = {}


def _get_builder(NT, layers, stop=None):
    key = (NT, tuple(layers), stop)
    if key not in _CACHE:
        b = Builder(NT, list(layers), stop)
        b.build()
        _CACHE[key] = b
    return _CACHE[key]


def _pack_weights(b, inp):
    ws = np.zeros((len(b.layers) * NBMAX, 128, 2048), np.float32)
    for l, specs in b.wspecs.items():
        assert len(specs) <= NBMAX, len(specs)
        li = b.layers.index(l)
        for bi, (P, E, spec) in enumerate(specs):
            arr = spec(inp)
            ws[li * NBMAX + bi, 0:arr.shape[0], 0:arr.shape[1]] = arr
    return ws


def _run(inp, NT=S // T, layers=(0, 1), stop=None, ncores=B):
    b = _get_builder(NT, layers, stop)
    print('blocks per layer', {l: len(v) for l, v in b.wspecs.items()}, 'sbuf bytes', b.total_bytes, 'nops', len(b.sc.ops), flush=True)
    inp = {k: np.asarray(v) for k, v in inp.items()}
    ws = _pack_weights(b, inp)
    c, pro, cb, oh, w2 = _host_consts(inp)
    x = inp["x"]
    in_maps = []
    for core in range(ncores):
        xb = x[core, :NT * T, :]
        xl = np.ascontiguousarray(xb.reshape(NT, T, KC, 128).transpose(0, 3, 2, 1)).reshape(NT, 128, KC * T)
        in_maps.append({"xin": xl, "ws": ws, "cst": c, "pro": pro, "cb": cb, "ohd": oh, "w2d": w2})
    res = run_bass_kernel_spmd(b.nc, in_maps, core_ids=list(range(ncores)))
    if stop:
        return [np.asarray(res.results[core]["dbg"]) for core in range(ncores)], [np.asarray(res.results[core]["out"]) for core in range(ncores)]
    outs = []
    for core in range(ncores):
        o = np.asarray(res.results[core]["out"]).reshape(NT, 128, KC, T).transpose(0, 3, 2, 1).reshape(NT * T, D)
        outs.append(o)
    return np.stack(outs, 0).astype(np.float32)


def kernel(**inputs):
    return _run(inputs)
```

```python
import math
import numpy as np
import concourse.bass as bass
import concourse.mybir as mybir
from concourse.bass_utils import run_bass_kernel_spmd

F32 = mybir.dt.float32
BF16 = mybir.dt.bfloat16
AF = mybir.ActivationFunctionType
ALU = mybir.AluOpType

D = 2048
F = 5632
S = 4096
B = 8
L = 2
T = 512
KC = D // 128
FC = F // 128
EPS = 1e-6
NSLOT = 8
NBMAX = 439
GRAN = 256


class Sched:
    def __init__(self):
        self.ops = []

    @staticmethod
    def grans(ap):
        if ap is None:
            return ()
        sp = str(ap.space)
        if "SB" not in sp and "PSUM" not in sp.upper():
            return ()
        esz = 2 if ap.dtype == BF16 else 4
        a = ap.ap
        pstep = a[0][0]
        off = ap.offset
        fstart = off % pstep if pstep > 0 else off
        ext = 1
        for st, cnt in a[1:]:
            ext += (cnt - 1) * abs(st)
        b0 = fstart * esz
        b1 = (fstart + ext) * esz
        nm = ap.tensor.name
        gr = GRAN if "SB" in sp else 2048
        return [(nm, g) for g in range(b0 // gr, (b1 - 1) // gr + 1)]

    def add(self, eng, fn, reads=(), writes=(), dma=None):
        r = []
        for ap in reads:
            r.extend(self.grans(ap))
        w = []
        for ap in writes:
            w.extend(self.grans(ap))
        self.ops.append((eng, fn, r, w, dma))
        return len(self.ops) - 1

    def resolve(self):
        ops = self.ops
        n = len(ops)
        last_w = {}
        readers = {}
        deps = [None] * n
        chan = [None] * n
        for i, (eng, fn, r, w, dma) in enumerate(ops):
            ch = ("dma:" + dma) if dma else eng
            chan[i] = ch
            d = set()
            for g in r:
                x = last_w.get(g)
                if x is not None:
                    d.add(x)
            for g in w:
                x = last_w.get(g)
                if x is not None:
                    d.add(x)
                rr = readers.get(g)
                if rr:
                    d.update(rr.values())
            d.discard(i)
            for g in r:
                readers.setdefault(g, {})[ch] = i
            for g in w:
                last_w[g] = i
                readers[g] = {}
            if eng == "pe" and not dma:
                d = {j for j in d if chan[j] != "pe"}
            deps[i] = d
        signal = [False] * n
        for i in range(n):
            for j in deps[i]:
                signal[j] = True
        val = [0] * n
        cnt = {}
        for i in range(n):
            ch = chan[i]
            if ch.startswith("dma:"):
                cnt[ch] = cnt.get(ch, 0) + 16
                val[i] = cnt[ch]
            elif signal[i]:
                cnt[ch] = cnt.get(ch, 0) + 1
                val[i] = cnt[ch]
        seen = {}
        waits = [None] * n
        for i in range(n):
            eng = ops[i][0]
            sd = seen.setdefault(eng, {})
            need = {}
            for j in deps[i]:
                ch = chan[j]
                if val[j] > need.get(ch, 0):
                    need[ch] = val[j]
            wl = []
            for ch, v in need.items():
                if sd.get(ch, 0) < v:
                    sd[ch] = v
                    wl.append((ch, v))
            waits[i] = wl
        self.chan, self.val, self.signal, self.waits = chan, val, signal, waits
        return sorted(set(chan))


def _bucket_table():
    n = np.arange(128)
    max_exact = 16
    nf = np.maximum(n, 1).astype(np.float32)
    large = max_exact + (np.log(nf / np.float32(max_exact)).astype(np.float32)
                         / np.float32(math.log(128 / max_exact)) * np.float32(32 - max_exact)).astype(np.int32)
    large = np.minimum(large, 31)
    return np.where(n < max_exact, n, large)


def _const_tables():
    bt = _bucket_table()
    k = np.arange(128)[:, None]
    q = np.arange(128)[None, :]
    dist_prev = 128 + q - k
    dist_own = q - k
    valid = np.concatenate([dist_prev < 128, dist_own >= 0], axis=1)
    dist = np.concatenate([dist_prev, dist_own], axis=1)
    dist_c = np.clip(dist, 0, 127)
    bucket = bt[dist_c]
    oh = np.zeros((128, 32, 256), np.float32)
    for b in range(32):
        oh[:, b, :] = ((bucket == b) & valid)
    negm = np.where(valid, 0.0, -30000.0).astype(np.float32)
    s = np.arange(128)[:, None]
    t = np.arange(128)[None, :]
    gmask = ((s <= t) & ((s // 64) == (t // 64))).astype(np.float32)
    reset = np.ones(512, np.float32)
    reset[::64] = 0.0
    return oh, negm, gmask, reset


GAIN_NAMES = ["ffn1_pre_g", "ffn1_post_g", "mix_pre_g", "mix_post_g", "ffn2_pre_g", "ffn2_post_g"]
C_GAIN = 0
C_CONVW = C_GAIN + 6 * L * 16
C_CONVB = C_CONVW + L * 8 * 31
C_LNG = C_CONVB + L * 8
C_LNB = C_LNG + L * 8
C_GATEB = C_LNB + L * 8
C_GLAG = C_GATEB + L * 4
C_SINK = C_GLAG + L * 2
NCONST = C_SINK + L * 16
P_RB = 0
P_NEGM = 512
NPRO = 768
CB_ONES = 0
CB_IDENT = 128
CB_GMASK = 256
CB_RESET = 384
NCB = 896


def _host_consts(inp):
    c = np.zeros((128, NCONST), np.float32)
    p = np.arange(128)
    for gi, nm in enumerate(GAIN_NAMES):
        g = np.asarray(inp[nm], np.float32)
        for l in range(L):
            c[:, C_GAIN + (gi * L + l) * 16: C_GAIN + (gi * L + l) * 16 + 16] = g[l].reshape(16, 128).T
    cw = np.asarray(inp["conv_w"], np.float32)
    for l in range(L):
        c[:, C_CONVW + l * 248: C_CONVW + (l + 1) * 248] = cw[l].reshape(31, 8, 128).transpose(2, 1, 0).reshape(128, 248)
        for base, nm in ((C_CONVB, "conv_b"), (C_LNG, "conv_ln_g"), (C_LNB, "conv_ln_b")):
            c[:, base + l * 8: base + l * 8 + 8] = np.asarray(inp[nm], np.float32)[l].reshape(8, 128).T
        c[:, C_GATEB + l * 4: C_GATEB + l * 4 + 4] = np.asarray(inp["gla_gate_b"], np.float32)[l].reshape(4, 128).T
        c[:, C_GLAG + l * 2: C_GLAG + l * 2 + 2] = np.asarray(inp["gla_norm_g"], np.float32)[l].reshape(2, 128).T
        c[:, C_SINK + l * 16: C_SINK + l * 16 + 16] = np.asarray(inp["attn_sink"], np.float32)[l][None, :]
    oh, negm, gmask, reset = _const_tables()
    pro = np.zeros((128, NPRO), np.float32)
    pro[:, P_RB:P_RB + 512] = np.asarray(inp["rel_bias"], np.float32).reshape(1, 512)
    pro[:, P_NEGM:P_NEGM + 256] = negm
    cb = np.zeros((128, NCB), np.float32)
    cb[:, CB_ONES:CB_ONES + 128] = 1.0
    cb[:, CB_IDENT:CB_IDENT + 128] = np.eye(128, dtype=np.float32)
    cb[:, CB_GMASK:CB_GMASK + 128] = gmask
    cb[:, CB_RESET:CB_RESET + 512] = reset[None, :]
    w2 = np.zeros((16, L * 512), np.float32)
    for l in range(L):
        w2[:, l * 512:(l + 1) * 512] = np.asarray(inp["gla_gate_w2"], np.float32)[l]
    return c, pro, cb, oh.reshape(128, 32 * 256), w2


class Builder:
    def __init__(self, NT, layers, stop=None):
        self.NT = NT
        self.layers = layers
        self.stop = stop
        self.sc = Sched()
        self.wspecs = {}
        self.nc = bass.Bass("TRN2", target_bir_lowering=False)
        nc = self.nc
        self.xin = nc.dram_tensor("xin", [NT, 128, KC * T], F32, kind="ExternalInput").ap()
        self.ws = nc.dram_tensor("ws", [len(layers) * NBMAX, 128, 2048], F32, kind="ExternalInput").ap()
        self.dbg = nc.dram_tensor("dbg", [NT, 128, 32 * T], F32, kind="ExternalOutput").ap() if stop else None
        self.cst = nc.dram_tensor("cst", [128, NCONST], F32, kind="ExternalInput").ap()
        self.pro = nc.dram_tensor("pro", [128, NPRO], F32, kind="ExternalInput").ap()
        self.cb = nc.dram_tensor("cb", [128, NCB], F32, kind="ExternalInput").ap()
        self.ohd = nc.dram_tensor("ohd", [128, 32 * 256], F32, kind="ExternalInput").ap()
        self.w2d = nc.dram_tensor("w2d", [16, L * 512], F32, kind="ExternalInput").ap()
        self.out = nc.dram_tensor("out", [NT, 128, KC * T], F32, kind="ExternalOutput").ap()
        self.bank = 0
        self.bankset = None
        self.bankpos = {}
        self.slot = 0
        self.dmaid = 0

    def carve(self, off, dtype, shape, parts=128):
        n = 1
        for s in shape:
            n *= s
        esz = 2 if dtype == BF16 else 4
        assert off % 4 == 0
        ap = self.A[0:parts, off // 2: off // 2 + n * esz // 2]
        if dtype != BF16:
            ap = ap.bitcast(dtype)
        if len(shape) == 2:
            ap = ap.rearrange("p (a b) -> p a b", b=shape[1])
        elif len(shape) == 3:
            ap = ap.rearrange("p (a b c) -> p a b c", b=shape[1], c=shape[2])
        return ap

    def mm(self, out, lhsT, rhs, start, stop):
        self.sc.add("pe", lambda e: e.matmul(out, lhsT=lhsT, rhs=rhs, start=start, stop=stop),
                    reads=(lhsT, rhs), writes=(out,))

    def tr(self, out, in_, ident):
        self.sc.add("pe", lambda e: e.transpose(out, in_, ident), reads=(in_, ident), writes=(out,))

    def act(self, out, in_, func, bias=None, scale=None):
        kw = {}
        rd = [in_]
        if bias is not None:
            kw["bias"] = bias
            if not isinstance(bias, (int, float)):
                rd.append(bias)
        if scale is not None:
            kw["scale"] = scale
            if not isinstance(scale, (int, float)):
                rd.append(scale)
        self.sc.add("act", lambda e: e.activation(out=out, in_=in_, func=func, **kw), reads=rd, writes=(out,))

    def tt(self, out, in0, in1, op, eng="dve"):
        self.sc.add(eng, lambda e: e.tensor_tensor(out=out, in0=in0, in1=in1, op=op), reads=(in0, in1), writes=(out,))

    def ts(self, out, in0, s1, s2, op0, op1=None, eng="dve"):
        rd = [in0] + [s for s in (s1, s2) if s is not None and not isinstance(s, (int, float))]
        if op1 is None:
            self.sc.add(eng, lambda e: e.tensor_scalar(out=out, in0=in0, scalar1=s1, scalar2=None, op0=op0),
                        reads=rd, writes=(out,))
        else:
            self.sc.add(eng, lambda e: e.tensor_scalar(out=out, in0=in0, scalar1=s1, scalar2=s2, op0=op0, op1=op1),
                        reads=rd, writes=(out,))

    def stt(self, out, in0, scalar, in1, op0, op1):
        rd = [in0, in1] + ([] if isinstance(scalar, (int, float)) else [scalar])
        self.sc.add("dve", lambda e: e.scalar_tensor_tensor(out=out, in0=in0, scalar=scalar, in1=in1, op0=op0, op1=op1),
                    reads=rd, writes=(out,))

    def cp(self, out, in_, eng="dve"):
        if eng == "act":
            self.sc.add("act", lambda e: e.copy(out=out, in_=in_), reads=(in_,), writes=(out,))
        else:
            self.sc.add(eng, lambda e: e.tensor_copy(out=out, in_=in_), reads=(in_,), writes=(out,))

    def recip(self, out, in_):
        self.sc.add("dve", lambda e: e.reciprocal(out=out, in_=in_), reads=(in_,), writes=(out,))

    def memset(self, ap, v, eng="dve"):
        self.sc.add(eng, lambda e: e.memset(ap, v), writes=(ap,))

    def dma(self, eng, out, in_, key=None):
        if key is None:
            self.dmaid += 1
            key = "u%d" % self.dmaid
        return self.sc.add(eng, lambda e: e.dma_start(out=out, in_=in_), reads=(in_,), writes=(out,), dma=key)

    def nb(self):
        if self.bankset is not None:
            bs, key = self.bankset
            i = self.bankpos.get(key, 0)
            self.bankpos[key] = (i + 1) % len(bs)
            return bs[i]
        b = self.bank
        self.bank = (self.bank + 1) % 8
        return b

    def wnext(self, l, t, P, E, spec):
        if t == 0 or l not in self.wspecs or len(self.wspecs[l]) <= self.wi:
            self.wspecs.setdefault(l, []).append((P, E, spec))
        bi = self.wi
        self.wi += 1
        s = self.slot
        self.slot = (self.slot + 1) % NSLOT
        dst = self.WB[0:P, s, 0:E]
        self.dma("pool", dst, self.ws[self.layers.index(l) * NBMAX + bi, 0:P, 0:E], key="wb%d" % s)
        return dst

    def colsum_bcast(self, srcs, ps):
        n = len(srcs)
        for i, s_ in enumerate(srcs):
            self.mm(ps, self.ONES, s_, i == 0, i == n - 1)

    def rstd_from(self, ps, dst, inv_n):
        self.act(dst, ps, AF.Ln, bias=self.EPSC, scale=inv_n)
        self.act(dst, dst, AF.Exp, scale=-0.5)

    def gcol(self, gi, l, c):
        k = C_GAIN + (gi * L + l) * 16 + c
        return self.CONST[:, k:k + 1]

    def prenorm(self, l, gi):
        ps = self.PS[:, self.nb(), :]
        for c in range(KC):
            sq = self.SQ[:, c % 2, :]
            self.act(sq, self.X[:, c, :], AF.Square)
            self.mm(ps, self.ONES, sq, c == 0, c == KC - 1)
        self.rstd_from(ps, self.RSTD, 1.0 / D)
        for c in range(KC):
            self.stt(self.H[:, c, :], self.X[:, c, :], self.gcol(gi, l, c), self.RSTD, ALU.mult, ALU.mult)

    def postnorm_residual(self, l, gi, PN, coef, sqdone_ps):
        self.rstd_from(sqdone_ps, self.RSTD, 1.0 / D)
        for c in range(KC):
            tmp = self.TMP[:, c % 2, :]
            self.stt(tmp, PN[:, c, :], self.gcol(gi, l, c), self.RSTD, ALU.mult, ALU.mult)
            self.stt(self.X[:, c, :], tmp, float(coef), self.X[:, c, :], ALU.mult, ALU.add)

    def ffn(self, l, t, which):
        gi_pre, gi_post = (0, 1) if which == 1 else (4, 5)
        n_gate, n_up, n_down = (("ffn1_w_gate", "ffn1_w_up", "ffn1_w_down") if which == 1
                                else ("ffn2_w_gate", "ffn2_w_up", "ffn2_w_down"))
        self.prenorm(l, gi_pre)
        ACTB = self.carve(self.o_big, BF16, [FC, T])
        for f in range(FC):
            def spec_g(inp, l=l, f=f, nm=n_gate):
                return np.asarray(inp[nm][l][:, f * 128:(f + 1) * 128]).reshape(16, 128, 128).transpose(1, 0, 2).reshape(128, 2048)
            def spec_u(inp, l=l, f=f, nm=n_up):
                return np.asarray(inp[nm][l][:, f * 128:(f + 1) * 128]).reshape(16, 128, 128).transpose(1, 0, 2).reshape(128, 2048)
            wg = self.wnext(l, t, 128, 2048, spec_g)
            pg = self.PS[:, self.nb(), :]
            for k in range(KC):
                self.mm(pg, wg[:, k * 128:(k + 1) * 128], self.H[:, k, :], k == 0, k == KC - 1)
            wu = self.wnext(l, t, 128, 2048, spec_u)
            pu = self.PS[:, self.nb(), :]
            for k in range(KC):
                self.mm(pu, wu[:, k * 128:(k + 1) * 128], self.H[:, k, :], k == 0, k == KC - 1)
            sg = self.TMP[:, f % 2, :]
            self.act(sg, pg, AF.Silu)
            self.tt(ACTB[:, f, :], sg, pu, ALU.mult)
        PN = self.carve(self.o_h, F32, [KC, T])
        pss = self.PS[:, self.nb(), :]
        for c in range(KC):
            po = self.fresh_bank([pss])
            k0 = 0
            for part, nk in enumerate((16, 16, 12)):
                def spec_d(inp, l=l, c=c, k0=k0, nk=nk, nm=n_down):
                    w = np.asarray(inp[nm][l][k0 * 128:(k0 + nk) * 128, c * 128:(c + 1) * 128])
                    return w.reshape(nk, 128, 128).transpose(1, 0, 2).reshape(128, nk * 128)
                wd = self.wnext(l, t, 128, nk * 128, spec_d)
                for k in range(nk):
                    self.mm(po, wd[:, k * 128:(k + 1) * 128], ACTB[:, k0 + k, :], (k0 + k) == 0, (k0 + k) == FC - 1)
                k0 += nk
            self.cp(PN[:, c, :], po, eng="act")
            sq = self.SQ[:, c % 2, :]
            self.act(sq, po, AF.Square)
            if c > 0:
                self.mm(pss, self.ONES, self.SQ[:, (c - 1) % 2, :], c == 1, False)
        self.mm(pss, self.ONES, self.SQ[:, (KC - 1) % 2, :], False, True)
        self.postnorm_residual(l, gi_post, PN, 0.5, pss)

    def _bank_of(self, ps_ap):
        return (ps_ap.offset % ps_ap.ap[0][0]) // 512

    def fresh_bank(self, avoid):
        av = {self._bank_of(a) for a in avoid}
        while True:
            b = self.nb()
            if b not in av:
                return self.PS[:, b, :]

    def mixer(self, l, t):
        first = (t == 0)
        self.prenorm(l, 2)
        H = self.H
        o_ws = self.o_extra
        YA = self.carve(o_ws, BF16, [8, T])
        YB = self.carve(o_ws + 16384, BF16, [8, T])
        YC = self.carve(o_ws + 24576, BF16, [8, T])
        oW = o_ws + 32768
        WIN = "w_in"
        oQ, oK, oV, oCV, oQC, oKC, oVC, oGC, oLR, oGL = 0, 1024, 1280, 1536, 3584, 4096, 4608, 5632, 6656, 6672

        def colblock(col0, ncols=128):
            def spec(inp, l=l, col0=col0, ncols=ncols):
                w = np.asarray(inp[WIN][l][:, col0:col0 + ncols])
                return w.reshape(16, 128, ncols).transpose(1, 0, 2).reshape(128, 16 * ncols)
            return spec

        def proj_fm(col0, M=128, ncols=128, wsl=None):
            w = self.wnext(l, t, 128, 16 * ncols, colblock(col0, ncols))
            return w

        oAW = self.o_attw
        QA = self.carve(oAW, BF16, [2, T])
        KA = self.carve(oAW + 4096, BF16, [4, 640])
        VA = self.carve(oAW + 9216, BF16, [5, 256])
        E = self.carve(oAW + 11776, F32, [2, 256])
        P = self.carve(oAW + 13824, BF16, [2, 256])
        RD = self.carve(oAW + 14848, F32, [2, 128])
        for g in range(4):
            def spec_k(inp, l=l, g=g):
                w = np.asarray(inp[WIN][l][:, oK + g * 64: oK + (g + 1) * 64])
                w = np.concatenate([w, w], axis=1)
                return w.reshape(16, 128, 128).transpose(1, 0, 2).reshape(128, 2048)
            w = self.wnext(l, t, 128, 2048, spec_k)
            ps = self.PS[:, self.nb(), :]
            for k in range(KC):
                self.mm(ps, w[:, k * 128:(k + 1) * 128], H[:, k, :], k == 0, k == KC - 1)
            if not first:
                self.cp(KA[:, g, 0:128], self.KCARRY[:, l, g, :], eng="dve")
            self.cp(KA[:, g, 128:640], ps, eng="act")
            self.cp(self.KCARRY[:, l, g, :], KA[:, g, 512:640], eng="dve")
        vps = [self.PS[:, self.nb(), 0:256] for _ in range(4)]
        for half in range(2):
            def spec_v(inp, l=l, half=half):
                w = np.asarray(inp[WIN][l][half * 1024:(half + 1) * 1024, oV:oV + 256])
                return w.reshape(8, 128, 256).transpose(1, 0, 2).reshape(128, 2048)
            w = self.wnext(l, t, 128, 2048, spec_v)
            for tb in range(4):
                for k in range(8):
                    kk = half * 8 + k
                    self.mm(vps[tb], H[:, kk, tb * 128:(tb + 1) * 128], w[:, k * 256:(k + 1) * 256], kk == 0, kk == KC - 1)
        if not first:
            self.cp(VA[:, 0, :], self.VCARRY[:, l, :], eng="dve")
        for tb in range(4):
            self.cp(VA[:, 1 + tb, :], vps[tb], eng="act")
        self.cp(self.VCARRY[:, l, :], VA[:, 4, :], eng="dve")
        VCt = self.carve(o_ws + 8192, BF16, [4, 1024])
        GLR = self.carve(oW + 27136, BF16, [T], parts=16)
        def attn_gen():
            units = [(g, qb, hl) for g in range(4) for qb in range(4) for hl in range(4)]
            held = {}

            def qproj(g):
                for hp in range(2):
                    w = self.wnext(l, t, 128, 2048, colblock(oQ + (g * 4 + hp * 2) * 64))
                    ps = self.PS[:, self.nb(), :]
                    for k in range(KC):
                        self.mm(ps, w[:, k * 128:(k + 1) * 128], H[:, k, :], k == 0, k == KC - 1)
                    self.cp(QA[:, hp, :], ps, eng="act")

            def stage1(i):
                g, qb, hl = units[i]
                if qb == 0 and hl == 0:
                    qproj(g)
                noprev = first and qb == 0
                h = g * 4 + hl
                hp, hf = hl // 2, hl % 2
                p0, p1 = hf * 64, hf * 64 + 64
                i2 = i % 2
                sps = self.PS[:, self.nb(), 0:256]
                q_ = QA[p0:p1, hp, qb * 128:(qb + 1) * 128]
                if not noprev:
                    self.mm(sps[:, 0:128], KA[p0:p1, g, qb * 128:(qb + 1) * 128], q_, True, True)
                self.mm(sps[:, 128:256], KA[p0:p1, g, (qb + 1) * 128:(qb + 2) * 128], q_, True, True)
                c0 = 128 if noprev else 0
                self.act(E[:, i2, c0:256], sps[:, c0:256], AF.Exp, scale=0.125)
                self.tt(P[:, i2, c0:256], E[:, i2, c0:256], self.EB[:, h, c0:256], ALU.mult)

            def stage2(i):
                g, qb, hl = units[i]
                noprev = first and qb == 0
                h = g * 4 + hl
                hf = hl % 2
                p0, p1 = hf * 64, hf * 64 + 64
                i2 = i % 2
                ops_ = self.PS[p0:p1, self.nb(), 0:256]
                if not noprev:
                    self.mm(ops_[:, 0:128], VA[:, qb, g * 64:(g + 1) * 64], P[:, i2, 0:128], True, False)
                self.mm(ops_[:, 0:128], VA[:, qb + 1, g * 64:(g + 1) * 64], P[:, i2, 128:256], noprev, True)
                if not noprev:
                    self.mm(ops_[:, 128:256], self.ONES[:, 0:64], P[:, i2, 0:128], True, False)
                self.mm(ops_[:, 128:256], self.ONES[:, 0:64], P[:, i2, 128:256], noprev, True)
                rd = RD[p0:p1, i2, :]
                ks = C_SINK + l * 16 + h
                self.act(rd, ops_[:, 128:256], AF.Ln, bias=self.CONST[p0:p1, ks:ks + 1], scale=1.0)
                self.act(rd, rd, AF.Exp, scale=-1.0)
                self.tt(YA[p0:p1, h // 2, qb * 128:(qb + 1) * 128], ops_[:, 0:128], rd, ALU.mult)

            stage1(0)
            yield
            for i in range(len(units)):
                if i + 1 < len(units):
                    stage1(i + 1)
                stage2(i)
                yield

        CO = self.carve(oW, F32, [8, T])
        YBUF = self.carve(oW + 16384, BF16, [2, 544])
        SGB = self.carve(oW + 20736, F32, [2, T])
        NDG = 8
        DG = self.carve(oW + 24832, BF16, [NDG, 128])
        CCB = self.CCARRYB
        def conv_gen():
          for c in range(8):
              wa = self.wnext(l, t, 128, 2048, colblock(oCV + c * 128))
              pa = self.PS[:, self.nb(), :]
              for k in range(KC):
                  self.mm(pa, wa[:, k * 128:(k + 1) * 128], H[:, k, :], k == 0, k == KC - 1)
              wg = self.wnext(l, t, 128, 2048, colblock(oCV + 1024 + c * 128))
              pg = self.PS[:, self.nb(), :]
              for k in range(KC):
                  self.mm(pg, wg[:, k * 128:(k + 1) * 128], H[:, k, :], k == 0, k == KC - 1)
              yb = YBUF[:, c % 2, :]
              sg = SGB[:, c % 2, :]
              self.act(sg, pg, AF.Sigmoid)
              self.cp(yb[:, 0:30], CCB[:, l, c, :], eng="dve")
              self.tt(yb[:, 30:30 + T], sg, pa, ALU.mult)
              self.cp(CCB[:, l, c, :], yb[:, T:T + 30], eng="dve")
              yield
              kw = C_CONVW + l * 248 + c * 31
              kb = C_CONVB + l * 8 + c
              pc = self.PS[:, self.nb(), :]
              for j in range(31):
                  dg = DG[:, (c * 31 + j) % NDG, :]
                  self.ts(dg, self.IDENT, self.CONST[:, kw + j:kw + j + 1], None, ALU.mult)
                  self.mm(pc, dg, yb[:, j:j + T], j == 0, j == 30)
                  if j == 15:
                      yield
              self.act(CO[:, c, :], pc, AF.Identity, bias=self.CONST[:, kb:kb + 1], scale=1.0)
              yield

          for pas in range(2):
              vps = [self.PS[:, self.nb(), :] for _ in range(4)]
              for q4 in range(4):
                  def spec_vc(inp, l=l, pas=pas, q4=q4):
                      w = np.asarray(inp[WIN][l][q4 * 512:(q4 + 1) * 512, oVC + pas * 512: oVC + (pas + 1) * 512])
                      return w.reshape(4, 128, 512).transpose(1, 0, 2).reshape(128, 2048)
                  w = self.wnext(l, t, 128, 2048, spec_vc)
                  for tb in range(4):
                      for k in range(4):
                          kk = q4 * 4 + k
                          self.mm(vps[tb], H[:, kk, tb * 128:(tb + 1) * 128], w[:, k * 512:(k + 1) * 512], kk == 0, kk == KC - 1)
                  yield
              for tb in range(4):
                  self.cp(VCt[:, tb, pas * 512:(pas + 1) * 512], vps[tb], eng="act")
          w = self.wnext(l, t, 128, 256, colblock(oLR, 16))
          psl = self.PS[0:16, self.nb(), :]
          for k in range(KC):
              self.mm(psl, w[:, k * 16:(k + 1) * 16], H[:, k, :], k == 0, k == KC - 1)
          self.cp(GLR, psl, eng="act")
          yield

        def run_threads(ga, gb, ratio):
            da = db = False
            while not (da and db):
                if not da:
                    self.bankset = ([0, 1, 2, 3], "A")
                    for _ in range(ratio):
                        try:
                            next(ga)
                        except StopIteration:
                            da = True
                            break
                if not db:
                    self.bankset = ([4, 5, 6, 7], "B")
                    try:
                        next(gb)
                    except StopIteration:
                        db = True
            self.bankset = None
        if self.stop == "attn":
            for _ in attn_gen():
                pass
            return [YA]
        run_threads(attn_gen(), conv_gen(), 2)
        psm = self.PS[:, self.nb(), :]
        pss = self.PS[:, self.nb(), :]
        for c in range(8):
            cb_ = self.SQ[:, 0, :]
            sq = self.SQ[:, 1, :]
            self.cp(cb_, CO[:, c, :], eng="act")
            self.act(sq, CO[:, c, :], AF.Square)
            self.mm(psm, self.ONES, cb_, c == 0, c == 7)
            self.mm(pss, self.ONES, sq, c == 0, c == 7)
        MEAN = self.MEAN
        self.act(MEAN, psm, AF.Copy, scale=1.0 / 1024)
        msq = self.TMP[:, 0, :]
        self.tt(msq, MEAN, MEAN, ALU.mult)
        var = self.TMP[:, 1, :]
        self.stt(var, pss, 1.0 / 1024, msq, ALU.mult, ALU.subtract)
        self.act(self.RSTD, var, AF.Ln, bias=self.EPSC, scale=1.0)
        self.act(self.RSTD, self.RSTD, AF.Exp, scale=-0.5)
        for c in range(8):
            tmp = self.TMP[:, c % 2, :]
            self.tt(tmp, CO[:, c, :], MEAN, ALU.subtract)
            self.tt(tmp, tmp, self.RSTD, ALU.mult)
            kg = C_LNG + l * 8 + c
            kb = C_LNB + l * 8 + c
            self.act(YB[:, c, :], tmp, AF.Silu, bias=self.CONST[:, kb:kb + 1], scale=self.CONST[:, kg:kg + 1])
        if self.stop == "conv":
            return [YA, YB]

        SP = self.carve(oW + 8192, F32, [T])
        CS = self.carve(oW + 10240, F32, [T])
        EBm = self.carve(oW + 12288, F32, [T])
        QE = self.carve(oW + 14336, BF16, [T])
        KE = self.carve(oW + 15360, BF16, [T])
        KTL = self.carve(oW + 16384, BF16, [T])
        KT = self.carve(oW + 17408, BF16, [4, 128])
        ATT = self.carve(oW + 18432, BF16, [2, 128])
        O = self.carve(oW + 18944, F32, [2, T])
        SGC = self.carve(oW + 23040, F32, [2, T])
        for hd in range(4):
            psz = self.PS[:, self.nb(), :]
            self.mm(psz, self.W2[:, l * 512 + hd * 128: l * 512 + (hd + 1) * 128], GLR, True, True)
            kgb = C_GATEB + l * 4 + hd
            self.act(SP, psz, AF.Exp, bias=self.NEGB[:, l * 4 + hd: l * 4 + hd + 1], scale=-1.0)
            self.act(SP, SP, AF.Ln, bias=self.ONEC, scale=1.0)
            self.sc.add("dve", lambda e, CS=CS, SP=SP: e.tensor_tensor_scan(out=CS, data0=self.RESET, data1=SP, initial=0.0,
                                                                      op0=ALU.mult, op1=ALU.add),
                        reads=(self.RESET, SP), writes=(CS,))
            self.act(EBm, CS, AF.Exp, scale=-1.0 / 16)
            EN = SP
            self.act(EN, CS, AF.Exp, scale=1.0 / 16)
            wq = self.wnext(l, t, 128, 2048, colblock(oQC + hd * 128))
            pq = self.PS[:, self.nb(), :]
            for k in range(KC):
                self.mm(pq, wq[:, k * 128:(k + 1) * 128], H[:, k, :], k == 0, k == KC - 1)
            self.stt(QE, pq, float(128 ** -0.5), EBm, ALU.mult, ALU.mult)
            wk = self.wnext(l, t, 128, 2048, colblock(oKC + hd * 128))
            pk = self.PS[:, self.nb(), :]
            for k in range(KC):
                self.mm(pk, wk[:, k * 128:(k + 1) * 128], H[:, k, :], k == 0, k == KC - 1)
            self.tt(KE, pk, EN, ALU.mult)
            EB3 = EBm.rearrange("p (c j) -> p c j", j=64)
            dec = EB3[:, :, 63:64]
            self.tt(KTL.rearrange("p (c j) -> p c j", j=64), KE.rearrange("p (c j) -> p c j", j=64),
                    dec.to_broadcast([128, 8, 64]), ALU.mult)
            for dvc in range(2):
                wgc = self.wnext(l, t, 128, 2048, colblock(oGC + hd * 256 + dvc * 128))
                pgc = self.PS[:, self.nb(), :]
                for k in range(KC):
                    self.mm(pgc, wgc[:, k * 128:(k + 1) * 128], H[:, k, :], k == 0, k == KC - 1)
                self.act(SGC[:, dvc, :], pgc, AF.Silu)
            for tb in range(4):
                pt = self.PS[:, self.nb(), 0:64].bitcast(BF16)
                self.tr(pt, KTL[:, tb * 128:(tb + 1) * 128], self.IDENT)
                self.cp(KT[:, tb, :], pt, eng="act")
            ST = self.STATE[:, l, hd, :]
            pre = {}

            def gla_pre(tb):
                pkv = [self.PS[:, self.nb(), 0:256] for _ in range(2)]
                for ch in range(2):
                    self.mm(pkv[ch], KT[ch * 64:(ch + 1) * 64, tb, :],
                            VCt[ch * 64:(ch + 1) * 64, tb, hd * 256:(hd + 1) * 256], True, True)
                pat = self.PS[:, self.nb(), 0:128]
                self.mm(pat, KE[:, tb * 128:(tb + 1) * 128], QE[:, tb * 128:(tb + 1) * 128], True, True)
                at = ATT[:, tb % 2, :]
                self.tt(at, pat, self.GMASK, ALU.mult)
                pre[tb] = (pkv, at)

            gla_pre(0)
            for tb in range(4):
                if tb + 1 < 4:
                    gla_pre(tb + 1)
                pkv, at = pre[tb]
                SB0 = self.SB[:, (tb % 2) * 2, :]
                SB1 = self.SB[:, (tb % 2) * 2 + 1, :]
                skipA = first and tb == 0
                if not skipA:
                    self.cp(SB0, ST, eng="act")
                ca = tb * 2
                self.stt(ST, ST, EB3[:, ca, 63:64], pkv[0], ALU.mult, ALU.add)
                self.cp(SB1, ST, eng="act")
                po = self.PS[:, self.nb(), 0:256]
                for dvc in range(2):
                    o_ = po[:, dvc * 128:(dvc + 1) * 128]
                    self.mm(o_, VCt[:, tb, hd * 256 + dvc * 128: hd * 256 + (dvc + 1) * 128], at, True, False)
                    if not skipA:
                        self.mm(o_[:, 0:64], SB0[:, dvc * 128:(dvc + 1) * 128], QE[:, tb * 128: tb * 128 + 64], False, False)
                    self.mm(o_[:, 64:128], SB1[:, dvc * 128:(dvc + 1) * 128], QE[:, tb * 128 + 64: tb * 128 + 128], False, True)
                self.stt(ST, ST, EB3[:, ca + 1, 63:64], pkv[1], ALU.mult, ALU.add)
                for dvc in range(2):
                    self.cp(O[:, dvc, tb * 128:(tb + 1) * 128], po[:, dvc * 128:(dvc + 1) * 128], eng="act")
            if self.stop == "glad" and hd == 0:
                GD = self.carve(self.o_big + 30000 - 30000 % 4, BF16, [8, T])
                GD = self.carve(self.o_x, F32, [16, T])
                self.cp(GD[:, 0, :], CS, eng="act")
                self.cp(GD[:, 1, :], EBm, eng="act")
                self.cp(GD[:, 2, :], QE, eng="act")
                self.cp(GD[:, 3, :], KE, eng="act")
                self.cp(GD[:, 4, :], KTL, eng="act")
                self.cp(GD[:, 5, :], O[:, 0, :], eng="act")
                self.cp(GD[:, 6, :], O[:, 1, :], eng="act")
                self.cp(GD[:, 7, :], SGC[:, 0, :], eng="act")
                self.cp(GD[:, 8, :], KT.rearrange("p a b -> p (a b)"), eng="act")
                self.cp(GD[:, 9, :], VCt[:, :, 0:128], eng="act")
                self.cp(GD[:, 10, :], SP, eng="act")
                self.dma("sp", self.dbg[t].rearrange("p (c s) -> p c s", s=T)[:, 0:16, :], self.X, key="dbg")
            pss = self.PS[:, self.nb(), :]
            for dvc in range(2):
                sq = self.SQ[:, dvc, :]
                self.act(sq, O[:, dvc, :], AF.Square)
                self.mm(pss, self.ONES, sq, dvc == 0, dvc == 1)
            self.rstd_from(pss, self.RSTD, 1.0 / 256)
            for dvc in range(2):
                tmp = self.TMP[:, dvc, :]
                kg = C_GLAG + l * 2 + dvc
                self.stt(tmp, O[:, dvc, :], self.CONST[:, kg:kg + 1], self.RSTD, ALU.mult, ALU.mult)
                self.tt(YC[:, hd * 2 + dvc, :], tmp, SGC[:, dvc, :], ALU.mult)
        if self.stop == "gla":
            return [YA, YB, YC]
        if self.stop == "glad":
            return [YA]

        MG = self.carve(oW, BF16, [KC, T])
        SGT = self.carve(oW + 16384, F32, [2, T])
        ACC = self.carve(oW + 20480, F32, [2, T])
        for oc in range(KC):
            acc = ACC[:, oc % 2, :]
            for br in range(3):
                wgl = self.wnext(l, t, 128, 2048, colblock(oGL + br * D + oc * 128))
                pg = self.PS[:, self.nb(), :]
                for k in range(KC):
                    self.mm(pg, wgl[:, k * 128:(k + 1) * 128], H[:, k, :], k == 0, k == KC - 1)
                pu = self.PS[:, self.nb(), :]
                nm = "w_a_up" if br == 0 else ("w_b_up" if br == 1 else "w_c_up")
                def spec_bc(inp, l=l, oc=oc, nm=nm):
                    w = np.asarray(inp[nm][l][:, oc * 128:(oc + 1) * 128])
                    return w.reshape(8, 128, 128).transpose(1, 0, 2).reshape(128, 1024)
                wu = self.wnext(l, t, 128, 1024, spec_bc)
                Y = (YA, YB, YC)[br]
                for k in range(8):
                    self.mm(pu, wu[:, k * 128:(k + 1) * 128], Y[:, k, :], k == 0, k == 7)
                sg = SGT[:, br % 2, :]
                self.act(sg, pg, AF.Sigmoid)
                if br == 0:
                    self.tt(acc, sg, pu, ALU.mult)
                elif br == 1:
                    self.tt(sg, sg, pu, ALU.mult)
                    self.tt(acc, acc, sg, ALU.add)
                else:
                    self.tt(sg, sg, pu, ALU.mult)
                    self.tt(MG[:, oc, :], acc, sg, ALU.add)
        M = self.carve(o_ws, F32, [KC, T])
        pss = self.PS[:, self.nb(), :]
        for c in range(KC):
            def spec_o(inp, l=l, c=c):
                w = np.asarray(inp["w_out"][l][:, c * 128:(c + 1) * 128])
                return w.reshape(16, 128, 128).transpose(1, 0, 2).reshape(128, 2048)
            w = self.wnext(l, t, 128, 2048, spec_o)
            po = self.fresh_bank([pss])
            for k in range(KC):
                self.mm(po, w[:, k * 128:(k + 1) * 128], MG[:, k, :], k == 0, k == KC - 1)
            self.cp(M[:, c, :], po, eng="act")
            sq = self.SQ[:, c % 2, :]
            self.act(sq, po, AF.Square)
            if c > 0:
                self.mm(pss, self.ONES, self.SQ[:, (c - 1) % 2, :], c == 1, False)
        self.mm(pss, self.ONES, self.SQ[:, (KC - 1) % 2, :], False, True)
        self.postnorm_residual(l, 3, M, 1.0, pss)
        return None

    def build(self):
        nc = self.nc
        NT = self.NT
        o = 0
        self.o_x = o; o += KC * T * 4
        self.o_h = o; o += KC * T * 2
        self.o_extra = o; o += 16384
        self.o_big = o; o += FC * T * 2
        o_wb = o; o += NSLOT * 4096
        self.o_attw = o; o += 16384
        o_eb = o; o += 16 * 256 * 2
        o_sq = o; o += 2 * T * 2
        o_rstd = o; o += T * 4
        o_mean = o; o += T * 4
        o_tmp = o; o += 2 * T * 4
        o_state = o; o += L * 4 * 256 * 4
        o_sb = o; o += 4 * 256 * 2
        o_kc = o; o += L * 4 * 128 * 2
        o_vc = o; o += L * 256 * 2
        o_cc = o; o += L * 8 * 30 * 4
        o_cb = o; o += NCB * 2
        o_const = o; o += ((NCONST + 16) * 4 + 3) // 4 * 4
        o_w2 = o; o += L * 512 * 2
        o_negb = o; o += 8 * 4
        total = o
        self.total_bytes = total
        with (
            nc.sbuf_tensor("A", [128, total // 2], BF16) as A,
            nc.psum_tensor("PS", [128, 8, 512], F32) as PS,
        ):
            self.A = A
            self.PS = PS
            self.X = self.carve(self.o_x, F32, [KC, T])
            self.H = self.carve(self.o_h, BF16, [KC, T])
            self.WB = self.carve(o_wb, BF16, [NSLOT, 2048])
            self.EB = self.carve(o_eb, BF16, [16, 256])
            self.SQ = self.carve(o_sq, BF16, [2, T])
            self.RSTD = self.carve(o_rstd, F32, [T])
            self.MEAN = self.carve(o_mean, F32, [T])
            self.TMP = self.carve(o_tmp, F32, [2, T])
            self.STATE = self.carve(o_state, F32, [L, 4, 256])
            self.SB = self.carve(o_sb, BF16, [4, 256])
            self.KCARRY = self.carve(o_kc, BF16, [L, 4, 128])
            self.VCARRY = self.carve(o_vc, BF16, [L, 256])
            self.CCARRY = self.carve(o_cc, F32, [L, 8, 30])
            self.CCARRYB = self.carve(o_cc, BF16, [L, 8, 30])
            CB = self.carve(o_cb, BF16, [NCB])
            self.ONES = CB[:, CB_ONES:CB_ONES + 128]
            self.IDENT = CB[:, CB_IDENT:CB_IDENT + 128]
            self.GMASK = CB[:, CB_GMASK:CB_GMASK + 128]
            self.RESET = CB[:, CB_RESET:CB_RESET + 512]
            CONSTF = self.carve(o_const, F32, [NCONST + 16])
            self.CONST = CONSTF
            self.EPSC = CONSTF[:, NCONST:NCONST + 1]
            self.ONEC = CONSTF[:, NCONST + 1:NCONST + 2]
            self.W2 = self.carve(o_w2, BF16, [L * 512], parts=16)
            self.NEGB = self.carve(o_negb, F32, [8])

            self.dma("sp", CONSTF[:, 0:NCONST], self.cst[:, :])
            self.dma("pool", CB, self.cb[:, :])
            self.dma("pool", self.W2, self.w2d[:, :])
            oW = self.o_extra + 32768
            OH = self.carve(self.o_extra, BF16, [32, 256])
            PRO = self.carve(self.o_extra + 16384, F32, [NPRO])
            ACCB = self.carve(self.o_extra + 16384 + NPRO * 4, F32, [2, 256])
            for q4 in range(4):
                self.dma("pool", OH[:, q4 * 8:(q4 + 1) * 8, :], self.ohd[:, q4 * 2048:(q4 + 1) * 2048].rearrange("p (a b) -> p a b", b=256))
            self.dma("sp", PRO, self.pro[:, :])
            self.memset(self.EPSC, EPS)
            self.memset(self.ONEC, 1.0)
            self.memset(self.STATE, 0.0)
            self.memset(self.CCARRY, 0.0)
            for h in range(16):
                acc = ACCB[:, h % 2, :]
                self.stt(acc, OH[:, 0, :], PRO[:, P_RB + h: P_RB + h + 1], PRO[:, P_NEGM:P_NEGM + 256], ALU.mult, ALU.add)
                for b in range(1, 32):
                    self.stt(acc, OH[:, b, :], PRO[:, P_RB + b * 16 + h: P_RB + b * 16 + h + 1], acc, ALU.mult, ALU.add)
                self.act(self.EB[:, h, :], acc, AF.Exp)
            sk = CONSTF[:, C_SINK:C_SINK + L * 16]
            self.act(sk, sk, AF.Exp)
            self.act(self.NEGB, CONSTF[:, C_GATEB:C_GATEB + 8], AF.Copy, scale=-1.0)

            dbg = None
            for t in range(NT):
                xv = self.xin[t].rearrange("p (c s) -> p c s", s=T)
                for c in range(KC):
                    self.dma("sp", self.X[:, c, :], xv[:, c, :], key="xld%d" % c)
                for l in self.layers:
                    self.wi = 0
                    if self.stop == "none":
                        continue
                    if self.stop not in ("attn", "conv", "gla", "glad"):
                        self.ffn(l, t, 1)
                    if self.stop == "ffn1":
                        continue
                    dbg = self.mixer(l, t)
                    if dbg is not None:
                        continue
                    self.ffn(l, t, 2)
                if dbg is not None and self.stop == "glad":
                    dbg = None
                if dbg is not None:
                    YA_ = dbg[0]
                    dv = self.dbg[t].rearrange("p (c s) -> p c s", s=T)
                    for c in range(YA_.shape[1]):
                        self.cp(self.X[0:YA_.shape[0], c, :], YA_[:, c, :], eng="act")
                    self.dma("sp", dv[:, 0:16, :], self.X, key="dbg")
                    if len(dbg) > 1:
                        for j, Y_ in enumerate(dbg[1:]):
                            for c in range(8):
                                self.cp(self.X[:, j * 8 + c, :], Y_[:, c, :], eng="act")
                        self.dma("sp", dv[:, 16:32, :], self.X, key="dbg")
                ov = self.out[t].rearrange("p (c s) -> p c s", s=T)
                for c in range(KC):
                    self.dma("sp", ov[:, c, :], self.X[:, c, :], key="xst%d" % c)
            self.sc.add("sp", None, reads=(self.X,), writes=(self.X,))

            self.emit()
        return nc

    def emit(self):
        nc = self.nc
        sc = self.sc
        chans = sc.resolve()
        ops = sc.ops
        engmap = {"pe": "tensor", "act": "scalar", "dve": "vector", "pool": "gpsimd", "sp": "sync"}
        per_eng = {k: [] for k in engmap}
        for i, op in enumerate(ops):
            per_eng[op[0]].append(i)
        import contextlib
        with contextlib.ExitStack() as es:
            sems = {}
            for ch in chans:
                sems[ch] = es.enter_context(nc.semaphore("s_" + ch.replace(":", "_")))
            block = es.enter_context(nc.Block())

            def make(engname):
                idxs = per_eng[engname]

                def body(e):
                    for i in idxs:
                        eng, fn, r, w, dma = ops[i]
                        for ch, v in sc.waits[i]:
                            e.wait_ge(sems[ch], v)
                        if fn is None:
                            continue
                        ins = fn(e)
                        if dma:
                            ins.then_inc(sems[sc.chan[i]], 16)
                        elif sc.signal[i]:
                            ins.then_inc(sems[sc.chan[i]], 1)
                return body

            for engname, attr in engmap.items():
                if per_eng[engname]:
                    getattr(block, attr)(make(engname))


_CACHE = {}


def _get_builder(NT, layers, stop=None):
    key = (NT, tuple(layers), stop)
    if key not in _CACHE:
        b = Builder(NT, list(layers), stop)
        b.build()
        _CACHE[key] = b
    return _CACHE[key]


def _pack_weights(b, inp):
    ws = np.zeros((len(b.layers) * NBMAX, 128, 2048), np.float32)
    for l, specs in b.wspecs.items():
        assert len(specs) <= NBMAX, len(specs)
        li = b.layers.index(l)
        for bi, (P, E, spec) in enumerate(specs):
            arr = spec(inp)
            ws[li * NBMAX + bi, 0:arr.shape[0], 0:arr.shape[1]] = arr
    return ws


def _run(inp, NT=S // T, layers=(0, 1), stop=None, ncores=B):
    b = _get_builder(NT, layers, stop)
    print('blocks per layer', {l: len(v) for l, v in b.wspecs.items()}, 'sbuf bytes', b.total_bytes, 'nops', len(b.sc.ops), flush=True)
    inp = {k: np.asarray(v) for k, v in inp.items()}
    ws = _pack_weights(b, inp)
    c, pro, cb, oh, w2 = _host_consts(inp)
    x = inp["x"]
    in_maps = []
    for core in range(ncores):
        xb = x[core, :NT * T, :]
        xl = np.ascontiguousarray(xb.reshape(NT, T, KC, 128).transpose(0, 3, 2, 1)).reshape(NT, 128, KC * T)
        in_maps.append({"xin": xl, "ws": ws, "cst": c, "pro": pro, "cb": cb, "ohd": oh, "w2d": w2})
    res = run_bass_kernel_spmd(b.nc, in_maps, core_ids=list(range(ncores)))
    if stop:
        return [np.asarray(res.results[core]["dbg"]) for core in range(ncores)], [np.asarray(res.results[core]["out"]) for core in range(ncores)]
    outs = []
    for core in range(ncores):
        o = np.asarray(res.results[core]["out"]).reshape(NT, 128, KC, T).transpose(0, 3, 2, 1).reshape(NT * T, D)
        outs.append(o)
    return np.stack(outs, 0).astype(np.float32)


def kernel(**inputs):
    return _run(inputs)
```
